# Optimizing a Trainium2 kernel written in Bass

```python
import jax, jax.numpy as jnp
from jax import lax
import numpy as np


D_MODEL = 1024
BATCH = 32
SEQ = 2048
DEPTH = 4

MEM_LEN = 256
HEAD_DIM = D_MODEL // 16
N_RET_HEADS = 8
N_ATT_HEADS = 8
RET_WIDTH = N_RET_HEADS * HEAD_DIM
ATT_WIDTH = N_ATT_HEADS * HEAD_DIM
MIX_WIDTH = RET_WIDTH + ATT_WIDTH
IN_WIDTH = 4 * RET_WIDTH + 3 * ATT_WIDTH
RET_CHUNK = 128
DILATED_PATTERNS = ((128, 1), (512, 4), (2048, 16))
ATT_BLOCK = 64
N_MEM_HEADS = 4
MEM_HEAD_DIM = D_MODEL // N_MEM_HEADS
PEER_HEADS = 8
PEER_N_KEYS = 128
PEER_N_EXPERTS = PEER_N_KEYS * PEER_N_KEYS
PEER_TOPK = 16
PEER_QUERY_DIM = 256
PEER_HALF = PEER_QUERY_DIM // 2
PEER_TOKEN_BLOCK = 128
RMS_EPS = 1e-6
NEG_INF = -1e30

kernel_name = 'hybrid_retention_dilated_peer_encoder'


def rms_norm(x, gain):
    xf = x.astype(jnp.float32)
    y = xf * lax.rsqrt(jnp.mean(xf * xf, axis=-1, keepdims=True) + RMS_EPS)
    return (y * gain.astype(jnp.float32)).astype(x.dtype)


def split_heads(t, n_heads):
    b, s, _ = t.shape
    return t.reshape(b, s, n_heads, -1).transpose(0, 2, 1, 3)


def head_rms_norm(o, gain):
    b, h, s, d = o.shape
    of = o.astype(jnp.float32)
    y = of * lax.rsqrt(jnp.mean(of * of, axis=-1, keepdims=True) + RMS_EPS)
    y = y.transpose(0, 2, 1, 3).reshape(b, s, h * d)
    return (y * gain.astype(jnp.float32)).astype(o.dtype)


def alibi_slopes(n_heads):
    return 2.0 ** (-8.0 * (jnp.arange(n_heads, dtype=jnp.float32) + 1.0) / n_heads)


def retention_scan(q, k, v, log_gamma, include_diag):
    b, h, s, d = q.shape
    dt = q.dtype
    n_chunks = s // RET_CHUNK
    pos = jnp.arange(RET_CHUNK, dtype=jnp.float32)
    rel = pos[:, None] - pos[None, :]
    mask = (rel >= 0) if include_diag else (rel > 0)
    decay = jnp.where(mask[None], jnp.exp(log_gamma[:, None, None] * jnp.where(mask, rel, 0.0)[None]), 0.0).astype(dt)
    xi = jnp.exp(log_gamma[:, None] * (pos[None] + 1.0)).astype(dt)
    zeta = jnp.exp(log_gamma[:, None] * (RET_CHUNK - 1.0 - pos[None])).astype(dt)
    chunk_decay = jnp.exp(log_gamma * RET_CHUNK).astype(dt)

    def to_chunks(t):
        return jnp.moveaxis(t.reshape(b, h, n_chunks, RET_CHUNK, t.shape[-1]), 2, 0)

    def step(state, qkv):
        qc, kc, vc = qkv
        inter = jnp.einsum('bhcd,bhde->bhce', qc, state) * xi[None, :, :, None]
        scores = jnp.einsum('bhcd,bhsd->bhcs', qc, kc) * decay[None]
        out = inter + jnp.einsum('bhcs,bhse->bhce', scores, vc)
        state = state * chunk_decay[None, :, None, None] + jnp.einsum('bhsd,bhse->bhde', kc * zeta[None, :, :, None], vc)
        return state, out

    state0 = jnp.zeros((b, h, d, v.shape[-1]), dt)
    _, out = lax.scan(step, state0, (to_chunks(q), to_chunks(k), to_chunks(v)))
    return jnp.moveaxis(out, 0, 2).reshape(b, h, s, v.shape[-1])


def bidirectional_retention(q, k, v, log_gamma):
    fwd = retention_scan(q, k, v, log_gamma[0], True)
    bwd = retention_scan(jnp.flip(q, 2), jnp.flip(k, 2), jnp.flip(v, 2), log_gamma[1], False)
    return fwd + jnp.flip(bwd, 2)


def dilated_band_attention(q, k, v, slopes, dilation, half_span):
    b, h, s, d = q.shape
    dt = q.dtype
    sub_len = s // dilation
    n_blocks = -(-sub_len // ATT_BLOCK)
    padded_len = n_blocks * ATT_BLOCK

    def to_sub(t):
        return t.reshape(b, h, sub_len, dilation, d).transpose(0, 1, 3, 2, 4)

    qs = jnp.pad(to_sub(q), ((0, 0), (0, 0), (0, 0), (0, padded_len - sub_len), (0, 0)))
    kv_pad = ((0, 0), (0, 0), (0, 0), (half_span, padded_len - sub_len + half_span), (0, 0))
    ks = jnp.pad(to_sub(k), kv_pad)
    vs = jnp.pad(to_sub(v), kv_pad)
    key_width = ATT_BLOCK + 2 * half_span
    key_idx = jnp.arange(n_blocks)[:, None] * ATT_BLOCK + jnp.arange(key_width)[None, :]
    kb = ks[:, :, :, key_idx]
    vb = vs[:, :, :, key_idx]
    qb = qs.reshape(b, h, dilation, n_blocks, ATT_BLOCK, d)
    scores = jnp.einsum('bhrnqd,bhrnkd->bhrnqk', qb, kb).astype(jnp.float32) * (d ** -0.5)
    q_pos = jnp.arange(n_blocks)[:, None] * ATT_BLOCK + jnp.arange(ATT_BLOCK)[None, :]
    k_pos = key_idx - half_span
    rel = k_pos[:, None, :] - q_pos[:, :, None]
    valid = (jnp.abs(rel) <= half_span) & (k_pos[:, None, :] >= 0) & (k_pos[:, None, :] < sub_len)
    dist = (jnp.abs(rel) * dilation).astype(jnp.float32)
    bias = -slopes[:, None, None, None] * dist[None]
    scores = jnp.where(valid[None, None, None], scores + bias[None, :, None], NEG_INF)
    m = jnp.max(scores, axis=-1, keepdims=True)
    p = jnp.exp(scores - m)
    l = jnp.sum(p, axis=-1, keepdims=True)
    out = jnp.einsum('bhrnqk,bhrnkd->bhrnqd', (p / l).astype(dt), vb)
    lse = (m + jnp.log(l))[..., 0]
    out = out.reshape(b, h, dilation, padded_len, d)[:, :, :, :sub_len].transpose(0, 1, 3, 2, 4).reshape(b, h, s, d)
    lse = lse.reshape(b, h, dilation, padded_len)[..., :sub_len].transpose(0, 1, 3, 2).reshape(b, h, s)
    return out, lse


def dilated_mixture_attention(q, k, v):
    slopes = alibi_slopes(q.shape[1])
    outs, lses = [], []
    for window, dilation in DILATED_PATTERNS:
        o, l = dilated_band_attention(q, k, v, slopes, dilation, window // (2 * dilation))
        outs.append(o)
        lses.append(l)
    weights = jax.nn.softmax(jnp.stack(lses, 0), axis=0).astype(q.dtype)
    return jnp.sum(weights[..., None] * jnp.stack(outs, 0), axis=0)


def memory_cross_attention(h, mem_n, w_q, w_kv, w_o):
    b, s, _ = h.shape
    q = (h @ w_q).reshape(b, s, N_MEM_HEADS, MEM_HEAD_DIM)
    k, v = jnp.split(mem_n @ w_kv, 2, axis=-1)
    k = k.reshape(b, -1, N_MEM_HEADS, MEM_HEAD_DIM)
    v = v.reshape(b, -1, N_MEM_HEADS, MEM_HEAD_DIM)
    scores = jnp.einsum('bshd,bmhd->bhsm', q, k).astype(jnp.float32) * (MEM_HEAD_DIM ** -0.5)
    p = jax.nn.softmax(scores, axis=-1).astype(h.dtype)
    o = jnp.einsum('bhsm,bmhd->bshd', p, v).reshape(b, s, D_MODEL)
    return o @ w_o


def peer_ffn(h, w_query, sub_keys, expert_down, expert_up):
    b, s, dm = h.shape
    dt = h.dtype
    tokens = h.reshape(-1, PEER_TOKEN_BLOCK, dm)
    keys_f = sub_keys.astype(jnp.float32)

    def block(xb):
        q = (xb @ w_query).reshape(PEER_TOKEN_BLOCK, PEER_HEADS, 2, PEER_HALF).astype(jnp.float32)
        half_scores = jnp.einsum('thcd,ckd->thck', q, keys_f)
        top_s, top_i = lax.top_k(half_scores, PEER_TOPK)
        cand_s = top_s[:, :, 0, :, None] + top_s[:, :, 1, None, :]
        cand_i = top_i[:, :, 0, :, None] * PEER_N_KEYS + top_i[:, :, 1, None, :]
        best_s, best_pos = lax.top_k(cand_s.reshape(PEER_TOKEN_BLOCK, PEER_HEADS, -1), PEER_TOPK)
        expert_idx = jnp.take_along_axis(cand_i.reshape(PEER_TOKEN_BLOCK, PEER_HEADS, -1), best_pos, axis=-1)
        gates = jax.nn.softmax(best_s, axis=-1).astype(dt)
        u = expert_down[expert_idx]
        v = expert_up[expert_idx]
        act = jax.nn.gelu(jnp.einsum('thkd,td->thk', u, xb))
        return jnp.einsum('thk,thkd->td', gates * act, v)

    return lax.map(block, tokens).reshape(b, s, dm)


def setup_inputs(seed: int = 0) -> dict:
    key = jax.random.key(seed)
    ks = jax.random.split(key, 20)
    f32 = jnp.float32

    def normal(k, shape, scale):
        return jax.random.normal(k, shape, f32) * scale

    def gain(k, shape):
        return 1.0 + 0.05 * jax.random.normal(k, shape, f32)

    base_logit = jnp.log(2.0 ** (5.0 + jnp.arange(N_RET_HEADS, dtype=f32)) - 1.0)
    return {
        'x': normal(ks[0], (BATCH, SEQ, D_MODEL), 1.0),
        'mem': normal(ks[1], (BATCH, MEM_LEN, D_MODEL), 1.0),
        'norm_mix': gain(ks[2], (DEPTH, D_MODEL)),
        'w_in': normal(ks[3], (DEPTH, D_MODEL, IN_WIDTH), D_MODEL ** -0.5),
        'ret_decay_logit': base_logit[None, None, :] + 0.1 * jax.random.normal(ks[4], (DEPTH, 2, N_RET_HEADS), f32),
        'ret_norm_gain': gain(ks[5], (DEPTH, RET_WIDTH)),
        'att_norm_gain': gain(ks[6], (DEPTH, ATT_WIDTH)),
        'w_out': normal(ks[7], (DEPTH, MIX_WIDTH, D_MODEL), MIX_WIDTH ** -0.5),
        'norm_mem': gain(ks[8], (DEPTH, D_MODEL)),
        'norm_mem_kv': gain(ks[9], (DEPTH, D_MODEL)),
        'w_mem_q': normal(ks[10], (DEPTH, D_MODEL, D_MODEL), D_MODEL ** -0.5),
        'w_mem_kv': normal(ks[11], (DEPTH, D_MODEL, 2 * D_MODEL), D_MODEL ** -0.5),
        'w_mem_o': normal(ks[12], (DEPTH, D_MODEL, D_MODEL), D_MODEL ** -0.5),
        'norm_ffn': gain(ks[13], (DEPTH, D_MODEL)),
        'peer_w_query': normal(ks[14], (DEPTH, D_MODEL, PEER_HEADS * PEER_QUERY_DIM), D_MODEL ** -0.5),
        'peer_sub_keys': normal(ks[15], (DEPTH, 2, PEER_N_KEYS, PEER_HALF), PEER_HALF ** -0.5),
        'peer_expert_down': normal(ks[16], (DEPTH, PEER_N_EXPERTS, D_MODEL), D_MODEL ** -0.5),
        'peer_expert_up': normal(ks[17], (DEPTH, PEER_N_EXPERTS, D_MODEL), (PEER_HEADS * PEER_TOPK) ** -0.5),
        'norm_final': gain(ks[18], (D_MODEL,)),
    }


def reference(x, mem, norm_mix, w_in, ret_decay_logit, ret_norm_gain, att_norm_gain, w_out,
              norm_mem, norm_mem_kv, w_mem_q, w_mem_kv, w_mem_o, norm_ffn,
              peer_w_query, peer_sub_keys, peer_expert_down, peer_expert_up, norm_final):
    split_points = [RET_WIDTH, 2 * RET_WIDTH, 3 * RET_WIDTH, 4 * RET_WIDTH,
                    4 * RET_WIDTH + ATT_WIDTH, 4 * RET_WIDTH + 2 * ATT_WIDTH]
    for layer in range(DEPTH):
        h = rms_norm(x, norm_mix[layer])
        proj = h @ w_in[layer]
        rq, rk, rv, rg, aq, ak, av = jnp.split(proj, split_points, axis=-1)
        log_gamma = jax.nn.log_sigmoid(ret_decay_logit[layer].astype(jnp.float32))
        ret = bidirectional_retention(split_heads(rq, N_RET_HEADS),
                                      split_heads(rk, N_RET_HEADS) * (HEAD_DIM ** -0.5),
                                      split_heads(rv, N_RET_HEADS), log_gamma)
        ret_out = head_rms_norm(ret, ret_norm_gain[layer]) * jax.nn.silu(rg)
        att = dilated_mixture_attention(split_heads(aq, N_ATT_HEADS), split_heads(ak, N_ATT_HEADS),
                                        split_heads(av, N_ATT_HEADS))
        att_out = head_rms_norm(att, att_norm_gain[layer])
        x = x + jnp.concatenate([ret_out, att_out], axis=-1) @ w_out[layer]
        h = rms_norm(x, norm_mem[layer])
        mem_n = rms_norm(mem, norm_mem_kv[layer])
        x = x + memory_cross_attention(h, mem_n, w_mem_q[layer], w_mem_kv[layer], w_mem_o[layer])
        h = rms_norm(x, norm_ffn[layer])
        x = x + peer_ffn(h, peer_w_query[layer], peer_sub_keys[layer],
                         peer_expert_down[layer], peer_expert_up[layer])
    return rms_norm(x, norm_final)
```

```python
import numpy as np
from contextlib import ExitStack

import concourse.bass as bass
import concourse.mybir as mybir
from concourse.bass_utils import run_bass_kernel_spmd

F32 = mybir.dt.float32
BF16 = mybir.dt.bfloat16
I32 = mybir.dt.int32
U32 = mybir.dt.uint32
ALU = mybir.AluOpType
AF = mybir.ActivationFunctionType
AX = mybir.AxisListType


class Res:
    __slots__ = ("name", "last_w", "readers", "dma_sem", "dma_cnt", "excl", "old_sems")

    def __init__(self, name):
        self.name = name
        self.excl = False
        self.old_sems = []
        self.last_w = None
        self.readers = []
        self.dma_sem = None
        self.dma_cnt = 0

    def dma_snapshot(self):
        return tuple(self.old_sems) + ((self.dma_sem, self.dma_cnt),)


class Op:
    __slots__ = ("eng", "fn", "deps", "flag", "cnt", "dma_res", "dma_val", "dma_sem", "sem")

    def __init__(self, eng, fn):
        self.eng = eng
        self.fn = fn
        self.deps = []
        self.flag = False
        self.cnt = 0
        self.dma_res = None
        self.dma_val = 0


ENGS = ("pe", "act", "dve", "pool", "sp")
import os as _os
SEM_LIMIT = int(_os.environ.get("SEM_LIMIT", "24000"))


class Sched:
    def __init__(self, nc):
        self.nc = nc
        self.stack = ExitStack()
        self.root = self.stack
        self.ops = {e: [] for e in ENGS}
        self.nres = 0
        self.nsem = 0
        self.free_sems = []
        self.all_res = []
        self.bar_tok = None
        self.scopes = []

    def res(self, name=None):
        self.nres += 1
        r = Res("%s_%d" % (name or "r", self.nres))
        r.last_w = self.bar_tok
        self.all_res.append(r)
        return r

    def push(self):
        self.scopes.append((self.stack, len(self.all_res)))
        self.stack = ExitStack()

    def pop(self):
        self.barrier()
        self.stack.close()
        self.stack, n0 = self.scopes.pop()
        for r in self.all_res[n0:]:
            if r.dma_sem is not None and r.name.startswith("sb_"):
                self.free_sems.append((r.dma_sem, r.dma_cnt))
                r.dma_sem = None
        self.all_res = self.all_res[:n0]

    def barrier(self):
        for e in ENGS:
            o = self.op(e, (lambda eng: eng.nop()), reads=(), writes=list(self.all_res))
        self.bar_tok = ("op", o, None)

    def I(self, eng, name, reads, writes, *args, **kw):
        return self.op(eng, (lambda e: getattr(e, name)(*args, **kw)), reads=reads, writes=writes)

    def D(self, eng, dst, srcs, out, in_, **kw):
        return self.dma(eng, (lambda e: e.dma_start(out=out, in_=in_, **kw)), dst, srcs=srcs)

    def sb(self, name, shape, dtype):
        self.nres += 1
        name = "sb_%s_%d" % (name, self.nres)
        t = self.stack.enter_context(self.nc.sbuf_tensor(name, list(shape), dtype))
        t_res = self.res(name)
        return T(t, t_res)

    def ps(self, name, shape, dtype=F32):
        self.nres += 1
        name = "ps_%s_%d" % (name, self.nres)
        t = self.stack.enter_context(self.nc.psum_tensor(name, list(shape), dtype))
        r = self.res(name)
        r.excl = True
        return T(t, r)

    def dram(self, name, shape, dtype, kind="Internal"):
        t = self.nc.dram_tensor(name, list(shape), dtype, kind=kind)
        return T(t.ap(), self.res(name))

    def _dep_tokens(self, op, reads, writes):
        toks = []
        for r in reads:
            if r.last_w is not None:
                toks.append(r.last_w)
            if r.excl:
                toks.extend(tk for tk in r.readers if tk[0] == "op" and tk[1].eng != op.eng)
        for w in writes:
            if w.last_w is not None:
                toks.append(w.last_w)
            toks.extend(w.readers)
        for tk in toks:
            if tk[0] == "op":
                if tk[1].eng == op.eng and op.eng in ("pe", "sp"):
                    continue
                op.deps.append(tk)
            else:
                op.deps.append(("dma", tk[1], tk[1].dma_snapshot()))

    def op(self, eng, fn, reads=(), writes=()):
        reads = [r.res if isinstance(r, T) else r for r in reads]
        writes = [w.res if isinstance(w, T) else w for w in writes]
        o = Op(eng, fn)
        self._dep_tokens(o, reads, writes)
        tok = ("op", o, None)
        if fn is not None:
            for r in reads:
                r.readers.append(tok)
            for w in writes:
                w.last_w = tok
                w.readers = []
        self.ops[eng].append(o)
        return o

    def dma(self, eng, fn, dst, srcs=(), extra_reads=()):
        dres = dst.res if isinstance(dst, T) else dst
        reads = [r.res if isinstance(r, T) else r for r in list(srcs) + list(extra_reads)]
        o = Op(eng, fn)
        self._dep_tokens(o, reads, [dres])
        if dres.dma_sem is not None and dres.dma_cnt + 16 > SEM_LIMIT:
            dres.old_sems.append((dres.dma_sem, dres.dma_cnt))
            dres.dma_sem = None
        if dres.dma_sem is None:
            while self.free_sems and self.free_sems[-1][1] + 16 > SEM_LIMIT:
                self.free_sems.pop()
            if self.free_sems:
                dres.dma_sem, dres.dma_cnt = self.free_sems.pop()
            else:
                self.nsem += 1
                dres.dma_sem = self.root.enter_context(self.nc.semaphore("d%d_%s" % (self.nsem, dres.name)))
                dres.dma_cnt = 0
        dres.dma_cnt += 16
        o.dma_res = dres
        o.dma_val = dres.dma_cnt
        o.dma_sem = dres.dma_sem
        tok = ("dma", dres, dres.dma_cnt)
        for r in reads:
            r.readers.append(tok)
        dres.last_w = tok
        dres.readers = []
        self.ops[eng].append(o)
        return o

    def fence(self, eng, reads):
        return self.op(eng, None, reads=reads, writes=())

    def emit(self):
        nc = self.nc
        for e in ENGS:
            for o in self.ops[e]:
                for d in o.deps:
                    if d[0] == "op":
                        d[1].flag = True
        for e in ENGS:
            c = 0
            ep = 0
            sem = None
            for o in self.ops[e]:
                if o.flag:
                    if sem is None or c >= SEM_LIMIT:
                        ep += 1
                        sem = self.root.enter_context(nc.semaphore("s_%s_%d" % (e, ep)))
                        c = 0
                    c += 1
                    o.cnt = c
                    o.sem = sem
        ops = self.ops

        def run(e, engine):
            waited = {}
            for o in ops[e]:
                need = {}
                for d in o.deps:
                    if d[0] == "op":
                        pairs = ((d[1].sem, d[1].cnt),)
                    else:
                        pairs = d[2]
                    for (s_, v) in pairs:
                        k = id(s_)
                        if waited.get(k, 0) >= v:
                            continue
                        if k not in need or need[k][1] < v:
                            need[k] = (s_, v)
                for k, (s_, v) in need.items():
                    engine.wait_ge(s_, v)
                    waited[k] = v
                if o.fn is None:
                    continue
                ins = o.fn(engine)
                if o.dma_res is not None:
                    ins.then_inc(o.dma_sem, 16)
                elif o.flag:
                    ins.then_inc(o.sem, 1)

        with nc.Block() as block:
            @block.tensor
            def _(eng):
                run("pe", eng)

            @block.scalar
            def _(eng):
                run("act", eng)

            @block.vector
            def _(eng):
                run("dve", eng)

            @block.gpsimd
            def _(eng):
                run("pool", eng)

            @block.sync
            def _(eng):
                run("sp", eng)

    def close(self):
        self.stack.close()


class T:
    __slots__ = ("t", "res")

    def __init__(self, t, res):
        self.t = t
        self.res = res

    def __getitem__(self, k):
        return self.t[k]


import math

SEQ = 2048
DM = 1024
MEM = 256
TW = 3968
TOFF = 1920


class Dd:
    pass


def declare(S, NSEQ, dbg=False, big=True, depth=1):
    d = Dd()
    NT = NSEQ * SEQ
    ein = lambda n, shp, dt=F32: S.dram(n, [depth] + shp, dt, kind="ExternalInput")
    d.depth = depth
    d.x_in = S.dram("x", [NT, DM], F32, kind="ExternalInput")
    d.mem = S.dram("mem", [NSEQ * MEM, DM], F32, kind="ExternalInput")
    d.st = {}
    d.st["norm_mix"] = ein("norm_mix", [1, DM])
    d.st["w_in"] = ein("w_in", [DM, 3584])
    d.st["decay"] = ein("decay", [1, 16])
    d.st["ret_gain"] = ein("ret_gain", [64, 8])
    d.st["att_gain"] = ein("att_gain", [64, 8])
    d.st["w_out"] = ein("w_out", [DM, DM])
    d.st["norm_mem"] = ein("norm_mem", [1, DM])
    d.st["norm_mem_kv"] = ein("norm_mem_kv", [1, DM])
    d.st["w_mem_q"] = ein("w_mem_q", [DM, DM])
    d.st["w_mem_kv"] = ein("w_mem_kv", [DM, 2 * DM])
    d.st["w_mem_o"] = ein("w_mem_o", [DM, DM])
    d.st["norm_ffn"] = ein("norm_ffn", [1, DM])
    d.st["peer_wq"] = ein("peer_wq", [DM, 2048])
    d.st["sub_keys"] = ein("sub_keys", [256, 128])
    if big:
        d.st["e_down"] = ein("e_down", [16384, DM])
        d.st["e_up"] = ein("e_up", [16384, DM])
    d.norm_final = S.dram("norm_final", [1, DM], F32, kind="ExternalInput")
    d.xn = S.dram("xn", [NT, DM], F32, kind="ExternalOutput")
    kd = "ExternalOutput" if dbg else "Internal"
    d.xbuf = [S.dram("xbuf%d" % i, [NT, DM], F32, kind=("ExternalOutput" if (dbg and i == 0) else "Internal")) for i in range(2)]
    d.projT = [S.dram("projT%d" % s, [2560, SEQ], BF16, kind=kd) for s in range(NSEQ)]
    d.vd = S.dram("vd", [NT, DM], BF16, kind=kd)
    d.cT = S.dram("cT", [DM, NT], BF16, kind=kd)
    d.x2 = S.dram("x2", [NT, DM], F32, kind=kd)
    d.kt_d = S.dram("kt_d", [NSEQ, 128, 8, MEM], BF16)
    d.vm_d = S.dram("vm_d", [NSEQ, 128, 2, DM], BF16)
    return d


def layer_view(d, L):
    v = Dd()
    v.__dict__.update(d.__dict__)
    for k, t in d.st.items():
        setattr(v, k, T(t.t[L], t.res))
    last = (L == d.depth - 1)
    if "e_down" in d.st:
        v.e_down_all = T(d.st["e_down"].t.rearrange("l e c -> (l e) c"), d.st["e_down"].res)
        v.e_up_all = T(d.st["e_up"].t.rearrange("l e c -> (l e) c"), d.st["e_up"].res)
    v.ebase = float(L * 16384)
    v.x = d.x_in if L == 0 else d.xbuf[(L - 1) % 2]
    v.xo = d.xbuf[L % 2]
    v.last = last
    return v


def make_ident(S):
    ident = S.sb("ident", [128, 128], BF16)
    io_i = S.sb("idio_i", [128, 128], I32)
    io_f = S.sb("idio_f", [128, 128], F32)
    S.I("pool", "iota", [], [io_i], io_i[:], pattern=[[1, 128]], base=0, channel_multiplier=-1)
    S.I("dve", "tensor_copy", [io_i], [io_f], out=io_f[:], in_=io_i[:])
    S.I("dve", "tensor_scalar", [io_f], [ident], out=ident[:], in0=io_f[:], scalar1=0.0, scalar2=None, op0=ALU.is_equal)
    return ident


def load_w_bf16(S, name, src, kc, ncol, p=128):
    w = S.sb(name, [p, kc, ncol], BF16)
    v = src.t.rearrange("(k p) n -> p k n", p=p)
    for c0 in range(0, ncol, 512):
        c1 = min(ncol, c0 + 512)
        S.D("pool", w, [src], w[:, :, c0:c1], v[:, :, c0:c1])
    return w


def rms_tile(S, xin, xres, nsub, gb, hout, ssq, rs, junk):
    S.I("dve", "memset", [], [ssq], ssq[:], 0.0)
    for j in range(nsub):
        S.I("act", "activation", [xres], [junk, ssq], out=junk[:], in_=xin[:, j, :], func=AF.Square, accum_out=ssq[:, j:j + 1])
    S.I("act", "activation", [ssq], [rs], out=rs[:, 0:nsub], in_=ssq[:, 0:nsub], func=AF.Sqrt, bias=1e-6, scale=1.0 / DM)
    S.I("dve", "reciprocal", [rs], [rs], out=rs[:, 0:nsub], in_=rs[:, 0:nsub])
    for j in range(nsub):
        for ho in hout:
            S.I("dve", "scalar_tensor_tensor", [xres, rs, gb], [ho], out=ho[:, j, :], in0=xin[:, j, :], scalar=rs[:, j:j + 1], in1=gb[:], op0=ALU.mult, op1=ALU.mult)


def phase1(S, d, NSEQ, ident):
    S.push()
    w = load_w_bf16(S, "w_in", d.w_in, 8, 3584)
    gb = S.sb("gb1", [128, DM], F32)
    S.D("sp", gb, [d.norm_mix], gb[:], d.norm_mix[0:1, :].partition_broadcast(128))
    xs = [S.sb("xs%d" % i, [128, 4, DM], F32) for i in range(2)]
    junk = S.sb("junk1", [128, DM], BF16)
    ssq = [S.sb("ssq%d" % i, [128, 4], F32) for i in range(2)]
    rs = [S.sb("rs%d" % i, [128, 4], F32) for i in range(2)]
    h = S.sb("h1", [128, 4, DM], BF16)
    hT = [S.sb("hT%d" % i, [128, 8, 512], BF16) for i in range(2)]
    fo = [S.sb("fo%d" % i, [128, 512], BF16) for i in range(4)]
    pt = [S.ps("pt%d" % i, [128, 8, 128], BF16) for i in range(2)]
    pf = [S.ps("pf%d" % i, [128, 512], F32) for i in range(4)]
    fm = []
    for c in range(4):
        fm.append((c * 128, c * 128, False))
    for c in range(4):
        fm.append((512 + c * 128, 512 + c * 128, False))
    for c in range(4):
        fm.append((1536 + c * 128, 1024 + c * 128, True))
    for c in range(4):
        fm.append((2048 + c * 128, 1536 + c * 128, False))
    for c in range(4):
        fm.append((2560 + c * 128, 2048 + c * 128, False))
    nfo = 0
    npf = 0
    for s in range(NSEQ):
        for tt in range(4):
            i = s * 4 + tt
            x_ = xs[i % 2]
            hT_ = hT[i % 2]
            r0 = s * SEQ + tt * 512
            S.D("sp", x_, [d.x], x_[:], d.x.t[r0:r0 + 512, :].rearrange("(j p) c -> p j c", p=128))
            rms_tile(S, x_, x_, 4, gb, [h], ssq[i % 2], rs[i % 2], junk)
            for j in range(4):
                p_ = pt[j % 2]
                for k in range(8):
                    S.I("pe", "transpose", [h, ident], [p_], out=p_[:, k, :], in_=h[:, j, k * 128:(k + 1) * 128], identity=ident[:])
                if j % 2 == 0:
                    S.I("act", "copy", [p_], [hT_], out=hT_[:, :, j * 128:(j + 1) * 128], in_=p_[:])
                else:
                    S.I("dve", "tensor_copy", [p_], [hT_], out=hT_[:, :, j * 128:(j + 1) * 128], in_=p_[:])
            for (col, row, silu) in fm:
                p_ = pf[npf % 4]
                npf += 1
                f_ = fo[nfo % 4]
                nfo += 1
                for k in range(8):
                    S.I("pe", "matmul", [w, hT_], [p_], p_[:], lhsT=w[:, k, col:col + 128], rhs=hT_[:, k, :], start=(k == 0), stop=(k == 7))
                if silu:
                    S.I("act", "activation", [p_], [f_], out=f_[:], in_=p_[:], func=AF.Silu)
                elif nfo % 2 == 0:
                    S.I("act", "copy", [p_], [f_], out=f_[:], in_=p_[:])
                else:
                    S.I("dve", "tensor_copy", [p_], [f_], out=f_[:], in_=p_[:])
                S.D("sp", d.projT[s], [f_], d.projT[s][row:row + 128, tt * 512:(tt + 1) * 512], f_[:])
            for j in range(4):
                for vi, col in enumerate((1024, 3072)):
                    p_ = pf[npf % 4]
                    npf += 1
                    f_ = fo[nfo % 4]
                    nfo += 1
                    for k in range(8):
                        S.I("pe", "matmul", [w, hT_], [p_], p_[:], lhsT=hT_[:, k, j * 128:(j + 1) * 128], rhs=w[:, k, col:col + 512], start=(k == 0), stop=(k == 7))
                    if nfo % 2 == 0:
                        S.I("act", "copy", [p_], [f_], out=f_[:], in_=p_[:])
                    else:
                        S.I("dve", "tensor_copy", [p_], [f_], out=f_[:], in_=p_[:])
                    S.D("sp", d.vd, [f_], d.vd[r0 + j * 128:r0 + (j + 1) * 128, vi * 512:(vi + 1) * 512], f_[:])
    S.pop()


def phase2(S, d, NSEQ, heads=range(16), tts=range(4)):
    S.push()
    dl_i = S.sb("dl_i", [128, TW], I32)
    A = S.sb("tA", [128, TW], F32)
    B = S.sb("tB", [128, TW], F32)
    rpos = S.sb("rpos", [128, TW], F32)
    rneg = S.sb("rneg", [128, TW], F32)
    multb = S.sb("multb", [128, TW], BF16)
    tab = [S.sb("tab%d" % i, [128, TW], BF16) for i in range(2)]
    S.I("pool", "iota", [], [dl_i], dl_i[:], pattern=[[1, TW]], base=-TOFF, channel_multiplier=-1)
    S.I("dve", "tensor_copy", [dl_i], [A], out=A[:], in_=dl_i[:])
    S.I("dve", "tensor_scalar", [A], [rpos], out=rpos[:], in0=A[:], scalar1=0.0, scalar2=None, op0=ALU.max)
    S.I("dve", "tensor_scalar", [A], [rneg], out=rneg[:], in0=A[:], scalar1=-1.0, scalar2=0.0, op0=ALU.mult, op1=ALU.max)
    S.I("dve", "tensor_tensor", [rpos, rneg], [A], out=A[:], in0=rpos[:], in1=rneg[:], op=ALU.add)
    S.I("dve", "tensor_scalar", [A], [multb], out=multb[:], in0=A[:], scalar1=64.0, scalar2=None, op0=ALU.is_le)
    tmp_i = dl_i
    for (inv, lim) in ((0.25, 256.0), (0.0625, 1024.0)):
        S.I("dve", "tensor_scalar", [A], [B], out=B[:], in0=A[:], scalar1=inv, scalar2=None, op0=ALU.mult)
        S.I("dve", "tensor_copy", [B], [tmp_i], out=tmp_i[:], in_=B[:])
        S.I("dve", "tensor_tensor", [B, tmp_i], [B], out=B[:], in0=B[:], in1=tmp_i[:], op=ALU.is_equal)
        S.I("dve", "scalar_tensor_tensor", [A, B], [B], out=B[:], in0=A[:], scalar=lim, in1=B[:], op0=ALU.is_le, op1=ALU.mult)
        S.I("dve", "tensor_tensor", [multb, B], [multb], out=multb[:], in0=multb[:], in1=B[:], op=ALU.add)
    lg = S.sb("lg", [128, 16], F32)
    S.D("sp", lg, [d.decay], lg[:], d.decay[0:1, :].partition_broadcast(128))
    S.I("act", "activation", [lg], [lg], out=lg[:], in_=lg[:], func=AF.Exp, scale=-1.0)
    S.I("act", "activation", [lg], [lg], out=lg[:], in_=lg[:], func=AF.Ln, bias=1.0)
    S.I("dve", "tensor_scalar", [lg], [lg], out=lg[:], in0=lg[:], scalar1=-1.0, scalar2=None, op0=ALU.mult)
    rgain = S.sb("rgain", [64, 8], F32)
    again = S.sb("again", [64, 8], F32)
    S.D("sp", rgain, [d.ret_gain], rgain[:], d.ret_gain[:, :])
    S.D("sp", again, [d.att_gain], again[:], d.att_gain[:, :])
    ones64 = S.sb("ones64", [64, 64], BF16)
    S.I("dve", "memset", [], [ones64], ones64[:], 1.0)
    sel = S.sb("sel", [128, 64], BF16)
    S.I("dve", "memset", [], [sel], sel[:], 0.0)
    S.I("dve", "memset", [sel], [sel], sel[64:65, :], 1.0)
    qT = [S.sb("qT%d" % i, [64, SEQ], BF16) for i in range(2)]
    kT = [S.sb("kT%d" % i, [64, SEQ], BF16) for i in range(2)]
    gT = [S.sb("gT%d" % i, [64, SEQ], BF16) for i in range(2)]
    V = [S.sb("V%d" % i, [128, 16, 128], BF16) for i in range(2)]
    for i in range(2):
        S.I("dve", "memset", [], [V[i]], V[i][:], 0.0)
        S.I("dve", "memset", [V[i]], [V[i]], V[i][:, :, 64:65], 1.0)
    PT = [S.sb("PT%d" % i, [128, 512], BF16) for i in range(3)]
    E = [S.sb("E%d" % i, [128, 512], BF16) for i in range(2)]
    Osb = S.sb("Osb", [64, 512], F32)
    Ob = S.sb("Ob", [128, 512], BF16)
    S.I("dve", "memset", [], [Ob], Ob[:], 0.0)
    sq = S.sb("sq", [64, 512], BF16)
    t1 = S.sb("t1", [64, 512], F32)
    rstd = S.sb("rstd", [64, 512], F32)
    yb = S.sb("yb", [64, 512], F32)
    cTs = [S.sb("cTs%d" % i, [64, 512], BF16) for i in range(2)]
    psc = [S.ps("psc%d" % i, [128, 512], F32) for i in range(3)]
    po = [S.ps("po%d" % i, [128, 512], F32) for i in range(2)]
    pss = S.ps("pss", [64, 512], F32)
    pl = S.ps("pl", [64, 512], F32)
    LN8 = math.log(0.125)
    nb = 0
    nt = 0
    for s in range(NSEQ):
        for hh in heads:
            sl = (s * 16 + hh) % 2
            isret = hh < 8
            h = hh % 8
            if isret:
                qr, kr, gr, vc = h * 64, 512 + h * 64, 1024 + h * 64, h * 64
            else:
                qr, kr, gr, vc = 1536 + h * 64, 2048 + h * 64, None, 512 + h * 64
            pj = d.projT[s]
            S.D("sp", qT[sl], [pj], qT[sl][:], pj[qr:qr + 64, :])
            S.D("sp", kT[sl], [pj], kT[sl][:], pj[kr:kr + 64, :])
            if isret:
                S.D("sp", gT[sl], [pj], gT[sl][:], pj[gr:gr + 64, :])
            for c4 in range(4):
                S.D("sp", V[sl], [d.vd], V[sl][:, c4 * 4:(c4 + 1) * 4, 0:64], d.vd[s * SEQ + c4 * 512:s * SEQ + (c4 + 1) * 512, vc:vc + 64].rearrange("(c p) e -> p c e", p=128))
            tb = tab[sl]
            if isret:
                S.I("dve", "tensor_scalar", [rpos, lg], [B], out=B[:], in0=rpos[:], scalar1=lg[:, h:h + 1], scalar2=None, op0=ALU.mult)
                S.I("dve", "scalar_tensor_tensor", [rneg, lg, B], [B], out=B[:], in0=rneg[:], scalar=lg[:, 8 + h:9 + h], in1=B[:], op0=ALU.mult, op1=ALU.add)
                S.I("act", "activation", [B], [tb], out=tb[:], in_=B[:], func=AF.Exp, bias=LN8)
            else:
                slope = 2.0 ** (-(h + 1))
                S.I("act", "activation", [A], [tb], out=tb[:], in_=A[:], func=AF.Exp, scale=-slope)
                S.I("dve", "tensor_tensor", [tb, multb], [tb], out=tb[:], in0=tb[:], in1=multb[:], op=ALU.mult)
            M = 64 if isret else 128
            for tt in tts:
                T0 = tt * 512
                blocks = []
                for c in range(16):
                    S0 = c * 128
                    if (not isret) and (T0 - S0 - 127 > 1024 or T0 - S0 + 511 < -1024):
                        continue
                    blocks.append(c)
                po_ = po[nt % 2]
                nt += 1
                for bi, c in enumerate(blocks):
                    S0 = c * 128
                    u0 = T0 - S0 + TOFF
                    sc = psc[nb % 3]
                    pt_ = PT[nb % 3]
                    e_ = E[nb % 2]
                    nb += 1
                    S.I("pe", "matmul", [kT[sl], qT[sl]], [sc], sc[:], lhsT=kT[sl][:, S0:S0 + 128], rhs=qT[sl][:, T0:T0 + 512], start=True, stop=True)
                    if isret:
                        S.I("dve", "tensor_tensor", [sc, tb], [pt_], out=pt_[:], in0=sc[:], in1=tb[:, u0:u0 + 512], op=ALU.mult)
                    else:
                        S.I("act", "activation", [sc], [e_], out=e_[:], in_=sc[:], func=AF.Exp, scale=0.125)
                        S.I("dve", "tensor_tensor", [e_, tb], [pt_], out=pt_[:], in0=e_[:], in1=tb[:, u0:u0 + 512], op=ALU.mult)
                    S.I("pe", "matmul", [V[sl], pt_], [po_], po_[0:M, :], lhsT=V[sl][:, c, 0:M], rhs=pt_[:], start=(bi == 0), stop=(bi == len(blocks) - 1))
                import os as _os
                if _os.environ.get("NOEPI"):
                    continue
                gcol = rgain[:, h:h + 1] if isret else again[:, h:h + 1]
                gres = rgain if isret else again
                ct_ = cTs[nt % 2]
                S.I("act", "copy", [po_], [Osb], out=Osb[:], in_=po_[0:64, :])
                S.I("act", "activation", [po_], [sq], out=sq[:], in_=po_[0:64, :], func=AF.Square)
                S.I("pe", "matmul", [ones64, sq], [pss], pss[:], lhsT=ones64[:], rhs=sq[:], start=True, stop=True)
                if isret:
                    S.I("act", "activation", [pss], [rstd], out=rstd[:], in_=pss[:], func=AF.Sqrt, bias=1e-6, scale=1.0 / 64)
                else:
                    lvl = int(_os.environ.get("EPI", "9"))
                    if lvl >= 0:
                        S.I("act", "copy", [po_], [Ob], out=Ob[:], in_=po_[:])
                    if lvl >= 1:
                        S.I("pe", "matmul", [sel, Ob], [pl], pl[:], lhsT=sel[:], rhs=Ob[:], start=True, stop=True)
                    if lvl >= 2:
                        S.I("act", "activation", [pl], [t1], out=t1[:], in_=pl[:], func=AF.Square, scale=1e-3)
                    if lvl >= 3:
                        S.I("dve", "scalar_tensor_tensor", [pss, t1], [t1], out=t1[:], in0=pss[:], scalar=1.0 / 64, in1=t1[:], op0=ALU.mult, op1=ALU.add)
                    if lvl >= 4:
                        S.I("act", "activation", [t1], [rstd], out=rstd[:], in_=t1[:], func=AF.Sqrt)
                    if lvl < 9:
                        continue
                S.I("dve", "reciprocal", [rstd], [rstd], out=rstd[:], in_=rstd[:])
                if isret:
                    S.I("dve", "scalar_tensor_tensor", [Osb, gres, rstd], [yb], out=yb[:], in0=Osb[0:64, :], scalar=gcol, in1=rstd[:], op0=ALU.mult, op1=ALU.mult)
                    S.I("dve", "tensor_tensor", [yb, gT[sl]], [ct_], out=ct_[:], in0=yb[:], in1=gT[sl][:, T0:T0 + 512], op=ALU.mult)
                else:
                    S.I("dve", "scalar_tensor_tensor", [Osb, gres, rstd], [ct_], out=ct_[:], in0=Osb[0:64, :], scalar=gcol, in1=rstd[:], op0=ALU.mult, op1=ALU.mult)
                S.D("sp", d.cT, [ct_], d.cT[hh * 64:(hh + 1) * 64, s * SEQ + T0:s * SEQ + T0 + 512], ct_[:])
    S.pop()


def phase3a(S, d, NSEQ, ident):
    S.push()
    w_kv = load_w_bf16(S, "w_mkv", d.w_mem_kv, 8, 2 * DM)
    gkv = S.sb("g_kv", [128, DM], F32)
    S.D("sp", gkv, [d.norm_mem_kv], gkv[:], d.norm_mem_kv[0:1, :].partition_broadcast(128))
    junk = S.sb("junk3a", [128, DM], F32)
    ssq = S.sb("ssq3a", [128, 4], F32)
    rs = S.sb("rs3a", [128, 4], F32)
    mx = S.sb("mx", [128, 2, DM], F32)
    mh = S.sb("mh", [128, 2, DM], BF16)
    mT = S.sb("mT", [128, 8, MEM], BF16)
    KT = S.sb("KTs", [128, 8, MEM], BF16)
    VM = S.sb("VMs", [128, 2, DM], BF16)
    p_t = S.ps("p_t3a", [128, 8, 128], BF16)
    p_a = S.ps("p_a3a", [128, MEM], F32)
    p_y = S.ps("p_y3a", [128, DM], F32)
    for s in range(NSEQ):
        S.D("sp", mx, [d.mem], mx[:], d.mem.t[s * MEM:(s + 1) * MEM, :].rearrange("(j p) c -> p j c", p=128))
        rms_tile(S, mx, mx, 2, gkv, [mh], ssq, rs, junk)
        for j in range(2):
            for k in range(8):
                S.I("pe", "transpose", [mh, ident], [p_t], out=p_t[:, k, :], in_=mh[:, j, k * 128:(k + 1) * 128], identity=ident[:])
            S.I("act", "copy", [p_t], [mT], out=mT[:, :, j * 128:(j + 1) * 128], in_=p_t[:])
        for f in range(8):
            for k in range(8):
                S.I("pe", "matmul", [w_kv, mT], [p_a], p_a[:], lhsT=w_kv[:, k, f * 128:(f + 1) * 128], rhs=mT[:, k, :], start=(k == 0), stop=(k == 7))
            S.I("act", "copy", [p_a], [KT], out=KT[:, f, :], in_=p_a[:])
        for j in range(2):
            for hf in range(2):
                for k in range(8):
                    S.I("pe", "matmul", [w_kv, mT], [p_y], p_y[:, hf * 512:(hf + 1) * 512], lhsT=mT[:, k, j * 128:(j + 1) * 128], rhs=w_kv[:, k, DM + hf * 512:DM + (hf + 1) * 512], start=(k == 0), stop=(k == 7))
            S.I("dve", "tensor_copy", [p_y], [VM], out=VM[:, j, :], in_=p_y[:])
        S.D("sp", d.kt_d, [KT], d.kt_d[s], KT[:])
        S.D("sp", d.vm_d, [VM], d.vm_d[s], VM[:])
    S.pop()


def phase3b(S, d, NSEQ, ident, ntiles=None):
    S.push()
    NT = NSEQ * SEQ
    w_out = load_w_bf16(S, "w_out", d.w_out, 16, DM, p=64)
    w_q = load_w_bf16(S, "w_mq", d.w_mem_q, 8, DM)
    w_o = load_w_bf16(S, "w_mo", d.w_mem_o, 8, DM)
    gm = S.sb("g_mem", [128, DM], F32)
    S.D("sp", gm, [d.norm_mem], gm[:], d.norm_mem[0:1, :].partition_broadcast(128))
    junk = S.sb("junk3b", [128, DM], F32)
    ssq = S.sb("ssq3b", [128, 4], F32)
    rs = S.sb("rs3b", [128, 4], F32)
    p_y = [S.ps("p_y%d" % i, [128, DM], F32) for i in range(2)]
    p_t = S.ps("p_t", [128, 8, 128], BF16)
    p_a = S.ps("p_a", [128, 8, 128], F32)
    p_l = S.ps("p_l", [128, 4, 128], F32)
    ones = S.sb("ones3", [128, 128], BF16)
    S.I("dve", "memset", [], [ones], ones[:], 1.0)
    KT = S.sb("KT", [128, 8, MEM], BF16)
    VM = S.sb("VM", [128, 2, DM], BF16)
    xt = [S.sb("xt%d" % i, [128, 1, DM], F32) for i in range(2)]
    cTt = [S.sb("cTt%d" % i, [64, 16, 128], BF16) for i in range(2)]
    hb = S.sb("hb", [128, 1, DM], BF16)
    hT = S.sb("hT3", [128, 8, 128], BF16)
    qT = S.sb("qT3", [128, 8, 128], BF16)
    PTm = S.sb("PTm", [128, 8, 128], BF16)
    rL = S.sb("rL", [128, 4, 128], F32)
    oT = S.sb("oT", [128, 8, 128], BF16)
    ntl = NT // 128 if ntiles is None else ntiles
    for ti in range(ntl):
        s = ti // 16
        r0 = ti * 128
        ct_ = cTt[ti % 2]
        x_ = xt[ti % 2]
        if ti % 16 == 0:
            S.D("sp", KT, [d.kt_d], KT[:], d.kt_d[s])
            S.D("sp", VM, [d.vm_d], VM[:], d.vm_d[s])
        S.D("sp", x_, [d.x], x_[:, 0, :], d.x[r0:r0 + 128, :])
        for q4 in range(4):
            S.D("sp", ct_, [d.cT], ct_[:, q4 * 4:(q4 + 1) * 4, :], d.cT.t[q4 * 256:(q4 + 1) * 256, r0:r0 + 128].rearrange("(h e) t -> e h t", e=64))
        py = p_y[0]
        for hf in range(2):
            for hh in range(16):
                S.I("pe", "matmul", [ct_, w_out], [py], py[:, hf * 512:(hf + 1) * 512], lhsT=ct_[:, hh, :], rhs=w_out[:, hh, hf * 512:(hf + 1) * 512], start=(hh == 0), stop=(hh == 15))
        S.I("dve", "tensor_tensor", [py, x_], [x_], out=x_[:, 0, :], in0=py[:], in1=x_[:, 0, :], op=ALU.add)
        rms_tile(S, x_, x_, 1, gm, [hb], ssq, rs, junk)
        for k in range(8):
            S.I("pe", "transpose", [hb, ident], [p_t], out=p_t[:, k, :], in_=hb[:, 0, k * 128:(k + 1) * 128], identity=ident[:])
        S.I("act", "copy", [p_t], [hT], out=hT[:], in_=p_t[:])
        for f in range(8):
            for k in range(8):
                S.I("pe", "matmul", [w_q, hT], [p_a], p_a[:, f, :], lhsT=w_q[:, k, f * 128:(f + 1) * 128], rhs=hT[:, k, :], start=(k == 0), stop=(k == 7))
        S.I("act", "copy", [p_a], [qT], out=qT[:], in_=p_a[:])
        pb = p_y[1]
        pbv = pb[:].rearrange("p (a b) -> p a b", b=128)
        for h in range(4):
            for mc in range(2):
                for dc in range(2):
                    S.I("pe", "matmul", [KT, qT], [pb], pbv[:, h * 2 + mc, :], lhsT=KT[:, 2 * h + dc, mc * 128:(mc + 1) * 128], rhs=qT[:, 2 * h + dc, :], start=(dc == 0), stop=(dc == 1))
        S.I("act", "activation", [pb], [PTm], out=PTm[:], in_=pbv, func=AF.Exp, scale=1.0 / 16)
        for h in range(4):
            for dc in range(2):
                for mc in range(2):
                    S.I("pe", "matmul", [VM, PTm], [p_a], p_a[:, 2 * h + dc, :], lhsT=VM[:, mc, (2 * h + dc) * 128:(2 * h + dc + 1) * 128], rhs=PTm[:, h * 2 + mc, :], start=(mc == 0), stop=(mc == 1))
            for mc in range(2):
                S.I("pe", "matmul", [ones, PTm], [p_l], p_l[:, h, :], lhsT=ones[:], rhs=PTm[:, h * 2 + mc, :], start=(mc == 0), stop=(mc == 1))
        S.I("dve", "reciprocal", [p_l], [rL], out=rL[:], in_=p_l[:])
        for h in range(4):
            S.I("dve", "tensor_tensor", [p_a, rL], [oT], out=oT[:, 2 * h:2 * h + 2, :], in0=p_a[:, 2 * h:2 * h + 2, :], in1=rL[:, h:h + 1, :].to_broadcast([128, 2, 128]), op=ALU.mult)
        for hf in range(2):
            for f in range(8):
                S.I("pe", "matmul", [oT, w_o], [py], py[:, hf * 512:(hf + 1) * 512], lhsT=oT[:, f, :], rhs=w_o[:, f, hf * 512:(hf + 1) * 512], start=(f == 0), stop=(f == 7))
        S.I("dve", "tensor_tensor", [py, x_], [x_], out=x_[:, 0, :], in0=py[:], in1=x_[:, 0, :], op=ALU.add)
        S.D("sp", d.x2, [x_], d.x2[r0:r0 + 128, :], x_[:, 0, :])
    S.pop()


def phase3c(S, d, NSEQ, ident, ntiles=None):
    S.push()
    NT = NSEQ * SEQ
    import os
    SK = os.environ.get("SK", "")
    if "w" not in SK:
        w_pq = load_w_bf16(S, "w_pq", d.peer_wq, 8, 2048)
    gf = S.sb("g_ffn", [128, DM], F32)
    gn = S.sb("g_fin", [128, DM], F32)
    if "g" not in SK:
        S.D("sp", gf, [d.norm_ffn], gf[:], d.norm_ffn[0:1, :].partition_broadcast(128))
        S.D("sp", gn, [d.norm_final], gn[:], d.norm_final[0:1, :].partition_broadcast(128))
    junk = S.sb("junk3c", [128, DM], F32)
    ssq = S.sb("ssq3c", [128, 4], F32)
    rs = S.sb("rs3c", [128, 4], F32)
    p_t = S.ps("p_tc", [128, 8, 128], BF16)
    p_a = S.ps("p_ac", [128, 8, 128], F32)
    p_b = S.ps("p_bc", [128, 8, 128], F32)
    keysT = S.sb("keysT", [128, 2, 128], BF16)
    kf = S.sb("kf", [128, 2, 128], F32)
    kb = S.sb("kb", [128, 2, 128], BF16)
    import os
    SK = os.environ.get("SK", "")
    if "k" not in SK:
        S.D("sp", kf, [d.sub_keys], kf[:], d.sub_keys.t.rearrange("(c k) e -> k c e", k=128))
        S.I("dve", "tensor_copy", [kf], [kb], out=kb[:], in_=kf[:])
        for c in range(2):
            S.I("pe", "transpose", [kb, ident], [p_t], out=p_t[:, c, :], in_=kb[:, c, :], identity=ident[:])
        S.I("act", "copy", [p_t], [keysT], out=keysT[:], in_=p_t[:, 0:2, :])
    io4_i = S.sb("io4_i", [128, 16], I32)
    io4s = S.sb("io4", [128, 16, 16], F32)
    lo16s = S.sb("lo16", [128, 16, 16], F32)
    hi16s = S.sb("hi16", [128, 16, 16], F32)
    if "i" not in SK:
        S.I("pool", "iota", [], [io4_i], io4_i[:], pattern=[[1, 16]], base=0, channel_multiplier=0)
        S.I("dve", "tensor_copy", [io4_i], [io4s], out=io4s[:], in_=io4_i[:].unsqueeze(1).to_broadcast([128, 16, 16]))
        S.I("dve", "tensor_scalar", [io4s], [lo16s], out=lo16s[:], in0=io4s[:], scalar1=16.0, scalar2=None, op0=ALU.mult)
        S.I("dve", "tensor_scalar", [io4s], [hi16s], out=hi16s[:], in0=io4s[:], scalar1=16.0, scalar2=16.0, op0=ALU.mult, op1=ALU.add)
    B4 = [128, 8, 16, 16]
    io4 = T(io4s[:].unsqueeze(1).to_broadcast(B4), io4s.res)
    lo16 = T(lo16s[:].unsqueeze(1).to_broadcast(B4), lo16s.res)
    hi16 = T(hi16s[:].unsqueeze(1).to_broadcast(B4), hi16s.res)

    xt = [S.sb("xc%d" % i, [128, 1, DM], F32) for i in range(2)]
    xnb = S.sb("xnb", [128, 1, DM], F32)
    hb = S.sb("hbc", [128, 1, DM], BF16)
    hf3 = S.sb("hf3", [128, 1, DM], F32)
    hT = S.sb("hTc", [128, 8, 128], BF16)
    qpT = S.sb("qpT", [128, 16, 128], BF16)
    Ssb = S.sb("Ssb", [128, 16, 128], F32)
    S2 = [S.sb("S2_%d" % i, [128, 128], F32) for i in range(2)]
    Tv = S.sb("Tv", [128, 16, 16], F32)
    Ti = S.sb("Ti", [128, 16, 16], U32)
    Tif = S.sb("Tif", [128, 16, 16], F32)
    cand = S.sb("cand", [128, 8, 16, 16], F32)
    cand2 = [S.sb("cand2_%d" % i, [128, 256], F32) for i in range(2)]
    Bv = S.sb("Bv", [128, 8, 16], F32)
    Bp = S.sb("Bp", [128, 8, 16], U32)
    Af = S.sb("Af", [128, 8, 16], F32)
    eq = S.sb("eq", [128, 8, 16, 16], F32)
    eq2 = S.sb("eq2", [128, 8, 16, 16], F32)
    Akf = S.sb("Akf", [128, 8, 16], F32)
    Bf = S.sb("Bf", [128, 8, 16], F32)
    Ik = S.sb("Ik", [128, 8, 16], F32)
    Jk = S.sb("Jk", [128, 8, 16], F32)
    Ef = S.sb("Ef", [128, 128], F32)
    Ei = S.sb("Ei", [128, 128], U32)
    Gt = S.sb("Gt", [128, 8, 16], F32)
    sm = S.sb("sm", [128, 8], F32)
    av = S.sb("av", [128, 128], F32)
    g1 = S.sb("g1", [128, 128], F32)
    g2 = S.sb("g2", [128, 128], F32)
    Wt = S.sb("Wt", [128, 128], F32)
    NROW = 8
    rows = [S.sb("row%d" % i, [128, DM], F32) for i in range(NROW)]
    nrow = 0
    import os
    CUT = int(os.environ.get("CUT", "99"))
    ntl = NT // 128 if ntiles is None else ntiles
    for ti in range(ntl):
        r0 = ti * 128
        x2 = xt[ti % 2]
        if CUT < 1:
            continue
        S.D("sp", x2, [d.x2], x2[:, 0, :], d.x2[r0:r0 + 128, :])
        rms_tile(S, x2, x2, 1, gf, [hb, hf3], ssq, rs, junk)
        for k in range(8):
            S.I("pe", "transpose", [hb, ident], [p_t], out=p_t[:, k, :], in_=hb[:, 0, k * 128:(k + 1) * 128], identity=ident[:])
        S.I("act", "copy", [p_t], [hT], out=hT[:], in_=p_t[:])
        for half, pp in enumerate((p_a, p_b)):
            for f in range(8):
                fc = half * 8 + f
                for k in range(8):
                    S.I("pe", "matmul", [w_pq, hT], [pp], pp[:, f, :], lhsT=w_pq[:, k, fc * 128:(fc + 1) * 128], rhs=hT[:, k, :], start=(k == 0), stop=(k == 7))
            if half == 0:
                S.I("act", "copy", [pp], [qpT], out=qpT[:, 0:8, :], in_=pp[:])
            else:
                S.I("dve", "tensor_copy", [pp], [qpT], out=qpT[:, 8:16, :], in_=pp[:])
        for half, pp in enumerate((p_a, p_b)):
            for f in range(8):
                hc = half * 8 + f
                S.I("pe", "matmul", [qpT, keysT], [pp], pp[:, f, :], lhsT=qpT[:, hc, :], rhs=keysT[:, hc % 2, :], start=True, stop=True)
            if half == 0:
                S.I("act", "copy", [pp], [Ssb], out=Ssb[:, 0:8, :], in_=pp[:])
            else:
                S.I("dve", "tensor_copy", [pp], [Ssb], out=Ssb[:, 8:16, :], in_=pp[:])
        if CUT < 2:
            continue
        for hc in range(16):
            s2 = S2[hc % 2]
            S.I("dve", "max", [Ssb], [Tv], out=Tv[:, hc, 0:8], in_=Ssb[:, hc, :])
            S.I("dve", "max_index", [Ssb, Tv], [Ti], out=Ti[:, hc, 0:8], in_max=Tv[:, hc, 0:8], in_values=Ssb[:, hc, :])
            S.I("dve", "match_replace", [Ssb, Tv], [s2], out=s2[:], in_to_replace=Tv[:, hc, 0:8], in_values=Ssb[:, hc, :], imm_value=-1e30)
            S.I("dve", "max", [s2], [Tv], out=Tv[:, hc, 8:16], in_=s2[:])
            S.I("dve", "max_index", [s2, Tv], [Ti], out=Ti[:, hc, 8:16], in_max=Tv[:, hc, 8:16], in_values=s2[:])
        if CUT < 3:
            continue
        S.I("dve", "tensor_copy", [Ti], [Tif], out=Tif[:], in_=Ti[:])
        Tv4 = Tv[:].rearrange("p (h c) k -> p h c k", c=2)
        Tif4 = Tif[:].rearrange("p (h c) k -> p h c k", c=2)
        S.I("dve", "tensor_tensor", [Tv], [cand], out=cand[:], in0=Tv4[:, :, 0, :].unsqueeze(3).to_broadcast(B4), in1=Tv4[:, :, 1, :].unsqueeze(2).to_broadcast(B4), op=ALU.add)
        for h in range(8):
            cf = cand[:, h, :, :].rearrange("p a b -> p (a b)")
            c2 = cand2[h % 2]
            S.I("dve", "max", [cand], [Bv], out=Bv[:, h, 0:8], in_=cf)
            S.I("dve", "max_index", [cand, Bv], [Bp], out=Bp[:, h, 0:8], in_max=Bv[:, h, 0:8], in_values=cf)
            S.I("dve", "match_replace", [cand, Bv], [c2], out=c2[:], in_to_replace=Bv[:, h, 0:8], in_values=cf, imm_value=-1e30)
            S.I("dve", "max", [c2], [Bv], out=Bv[:, h, 8:16], in_=c2[:])
            S.I("dve", "max_index", [c2, Bv], [Bp], out=Bp[:, h, 8:16], in_max=Bv[:, h, 8:16], in_values=c2[:])
        if CUT < 4:
            continue
        S.I("dve", "tensor_copy", [Bp], [Af], out=Af[:], in_=Bp[:])
        posb = Af[:].unsqueeze(3).to_broadcast(B4)
        S.I("dve", "tensor_tensor", [Af, lo16], [eq], out=eq[:], in0=posb, in1=lo16.t, op=ALU.is_ge)
        S.I("dve", "tensor_tensor", [Af, hi16], [eq2], out=eq2[:], in0=posb, in1=hi16.t, op=ALU.is_ge)
        S.I("dve", "tensor_tensor", [eq, eq2], [eq], out=eq[:], in0=eq[:], in1=eq2[:], op=ALU.subtract)
        S.I("dve", "tensor_tensor", [eq, io4], [eq2], out=eq2[:], in0=eq[:], in1=io4.t, op=ALU.mult)
        S.I("dve", "tensor_reduce", [eq2], [Akf], out=Akf[:], in_=eq2[:], axis=AX.X, op=ALU.add)
        S.I("dve", "tensor_tensor", [eq, Tif], [eq], out=eq[:], in0=eq[:], in1=Tif4[:, :, 0, :].unsqueeze(2).to_broadcast(B4), op=ALU.mult)
        S.I("dve", "tensor_reduce", [eq], [Ik], out=Ik[:], in_=eq[:], axis=AX.X, op=ALU.add)
        S.I("dve", "scalar_tensor_tensor", [Akf, Af], [Bf], out=Bf[:].rearrange("p h k -> p (h k)"), in0=Akf[:].rearrange("p h k -> p (h k)"), scalar=-16.0, in1=Af[:].rearrange("p h k -> p (h k)"), op0=ALU.mult, op1=ALU.add)
        S.I("dve", "tensor_tensor", [io4, Bf], [eq], out=eq[:], in0=io4.t, in1=Bf[:].unsqueeze(3).to_broadcast(B4), op=ALU.is_equal)
        S.I("dve", "tensor_tensor", [eq, Tif], [eq], out=eq[:], in0=eq[:], in1=Tif4[:, :, 1, :].unsqueeze(2).to_broadcast(B4), op=ALU.mult)
        S.I("dve", "tensor_reduce", [eq], [Jk], out=Jk[:], in_=eq[:], axis=AX.X, op=ALU.add)
        S.I("dve", "scalar_tensor_tensor", [Ik, Jk], [Ef], out=Ef[:], in0=Ik[:].rearrange("p h k -> p (h k)"), scalar=128.0, in1=Jk[:].rearrange("p h k -> p (h k)"), op0=ALU.mult, op1=ALU.add)
        if d.ebase:
            S.I("dve", "tensor_scalar", [Ef], [Ef], out=Ef[:], in0=Ef[:], scalar1=d.ebase, scalar2=None, op0=ALU.add)
        S.I("dve", "tensor_copy", [Ef], [Ei], out=Ei[:], in_=Ef[:])
        if CUT < 5:
            continue
        S.I("dve", "tensor_tensor", [Bv], [Gt], out=Gt[:], in0=Bv[:], in1=Bv[:, :, 0:1].to_broadcast([128, 8, 16]), op=ALU.subtract)
        S.I("act", "activation", [Gt], [Gt], out=Gt[:], in_=Gt[:], func=AF.Exp)
        S.I("dve", "tensor_reduce", [Gt], [sm], out=sm[:], in_=Gt[:], axis=AX.X, op=ALU.add)
        S.I("dve", "reciprocal", [sm], [sm], out=sm[:], in_=sm[:])
        S.I("dve", "tensor_tensor", [Gt, sm], [Gt], out=Gt[:], in0=Gt[:], in1=sm[:].unsqueeze(2).to_broadcast([128, 8, 16]), op=ALU.mult)
        if CUT < 6:
            continue
        S.I("dve", "memset", [], [av], av[:], 0.0)
        for c in range(128):
            rw = rows[nrow % NROW]
            nrow += 1
            S.dma("pool", (lambda e, rw=rw, c=c: e.indirect_dma_start(out=rw[:], out_offset=None, in_=d.e_down_all[:, :], in_offset=bass.IndirectOffsetOnAxis(ap=Ei[:, c:c + 1], axis=0))), rw, srcs=[Ei, d.e_down_all])
            S.I("dve", "scalar_tensor_tensor", [rw, hf3], [junk, av], out=junk[:], in0=rw[:], scalar=1.0, in1=hf3[:, 0, :], op0=ALU.mult, op1=ALU.mult, accum_out=av[:, c:c + 1])
        if CUT < 7:
            continue
        S.I("dve", "tensor_tensor", [av], [g1], out=g1[:], in0=av[:], in1=av[:], op=ALU.mult)
        S.I("dve", "tensor_scalar", [g1], [g1], out=g1[:], in0=g1[:], scalar1=0.044715, scalar2=1.0, op0=ALU.mult, op1=ALU.add)
        S.I("dve", "tensor_tensor", [g1, av], [g1], out=g1[:], in0=g1[:], in1=av[:], op=ALU.mult)
        S.I("act", "activation", [g1], [g2], out=g2[:], in_=g1[:], func=AF.Sigmoid, scale=1.5957691216057308)
        S.I("dve", "tensor_tensor", [g2, av], [g2], out=g2[:], in0=g2[:], in1=av[:], op=ALU.mult)
        S.I("dve", "tensor_tensor", [g2, Gt], [Wt], out=Wt[:], in0=g2[:], in1=Gt[:].rearrange("p h k -> p (h k)"), op=ALU.mult)
        for c in range(128):
            rw = rows[nrow % NROW]
            nrow += 1
            S.dma("pool", (lambda e, rw=rw, c=c: e.indirect_dma_start(out=rw[:], out_offset=None, in_=d.e_up_all[:, :], in_offset=bass.IndirectOffsetOnAxis(ap=Ei[:, c:c + 1], axis=0))), rw, srcs=[Ei, d.e_up_all])
            S.I("dve", "scalar_tensor_tensor", [rw, Wt, x2], [x2], out=x2[:, 0, :], in0=rw[:], scalar=Wt[:, c:c + 1], in1=x2[:, 0, :], op0=ALU.mult, op1=ALU.add)
        if d.last:
            rms_tile(S, x2, x2, 1, gn, [xnb], ssq, rs, junk)
            S.D("sp", d.xn, [xnb], d.xn[r0:r0 + 128, :], xnb[:, 0, :])
        else:
            S.D("sp", d.xo, [x2], d.xo[r0:r0 + 128, :], x2[:, 0, :])
    if d.last:
        S.fence("sp", [d.xn])
    S.pop()


def phase3(S, d, NSEQ, ident, ntiles=None):
    import os
    sub = os.environ.get("P3", "abc")
    if "a" in sub:
        phase3a(S, d, NSEQ, ident)
    if "b" in sub:
        phase3b(S, d, NSEQ, ident, ntiles)
    if "c" in sub:
        phase3c(S, d, NSEQ, ident, ntiles)


def build_prog(NSEQ, depth=4, phases=(1, 2, 3), ntiles=None, dbg=False, heads=range(16), tts=range(4)):
    nc = bass.Bass("TRN2", target_bir_lowering=False)
    S = Sched(nc)
    d = declare(S, NSEQ, dbg, big=(3 in phases), depth=depth)
    ident = make_ident(S)
    for L in range(depth):
        v = layer_view(d, L)
        if 1 in phases:
            phase1(S, v, NSEQ, ident)
        if 2 in phases:
            phase2(S, v, NSEQ, heads, tts)
        if 3 in phases:
            phase3(S, v, NSEQ, ident, ntiles)
    S.emit()
    print("[build] ops:", {e: len(S.ops[e]) for e in ENGS}, "dma sems:", S.nsem, flush=True)
    S.close()
    return nc


N_CORES = 8
DEPTH = 4
_NC_CACHE = {}


def _stacked_weights(inputs):
    g = lambda k: np.ascontiguousarray(np.asarray(inputs[k]), dtype=np.float32)
    D = g("norm_mix").shape[0]
    return {
        "norm_mix": g("norm_mix").reshape(D, 1, DM),
        "w_in": g("w_in"),
        "decay": g("ret_decay_logit").reshape(D, 1, 16),
        "ret_gain": np.ascontiguousarray(g("ret_norm_gain").reshape(D, 8, 64).transpose(0, 2, 1)),
        "att_gain": np.ascontiguousarray(g("att_norm_gain").reshape(D, 8, 64).transpose(0, 2, 1)),
        "w_out": g("w_out"),
        "norm_mem": g("norm_mem").reshape(D, 1, DM),
        "norm_mem_kv": g("norm_mem_kv").reshape(D, 1, DM),
        "w_mem_q": g("w_mem_q"),
        "w_mem_kv": g("w_mem_kv"),
        "w_mem_o": g("w_mem_o"),
        "norm_ffn": g("norm_ffn").reshape(D, 1, DM),
        "peer_wq": g("peer_w_query"),
        "sub_keys": g("peer_sub_keys").reshape(D, 256, 128),
        "e_down": g("peer_expert_down"),
        "e_up": g("peer_expert_up"),
        "norm_final": g("norm_final").reshape(1, DM),
    }


def kernel(**inputs):
    x = np.asarray(inputs["x"], dtype=np.float32)
    mem = np.asarray(inputs["mem"], dtype=np.float32)
    B = x.shape[0]
    nseq = B // N_CORES
    if nseq not in _NC_CACHE:
        _NC_CACHE[nseq] = build_prog(nseq, DEPTH)
    nc = _NC_CACHE[nseq]
    w = _stacked_weights(inputs)
    in_maps = []
    for c in range(N_CORES):
        m = dict(w)
        m["x"] = np.ascontiguousarray(x[c * nseq:(c + 1) * nseq].reshape(nseq * SEQ, DM))
        m["mem"] = np.ascontiguousarray(mem[c * nseq:(c + 1) * nseq].reshape(nseq * MEM, DM))
        in_maps.append(m)
    res = run_bass_kernel_spmd(nc, in_maps, core_ids=list(range(N_CORES)))
    out = np.concatenate([np.asarray(res.results[c]["xn"]).reshape(nseq, SEQ, DM) for c in range(N_CORES)], axis=0)
    return out.astype(np.float32)
```

```python
import numpy as np
from contextlib import ExitStack

import concourse.bass as bass
import concourse.mybir as mybir
from concourse.bass_utils import run_bass_kernel_spmd

F32 = mybir.dt.float32
BF16 = mybir.dt.bfloat16
I32 = mybir.dt.int32
U32 = mybir.dt.uint32
ALU = mybir.AluOpType
AF = mybir.ActivationFunctionType
AX = mybir.AxisListType


class Res:
    __slots__ = ("name", "last_w", "readers", "dma_sem", "dma_cnt", "excl", "old_sems")

    def __init__(self, name):
        self.name = name
        self.excl = False
        self.old_sems = []
        self.last_w = None
        self.readers = []
        self.dma_sem = None
        self.dma_cnt = 0

    def dma_snapshot(self):
        return tuple(self.old_sems) + ((self.dma_sem, self.dma_cnt),)


class Op:
    __slots__ = ("eng", "fn", "deps", "flag", "cnt", "dma_res", "dma_val", "dma_sem", "sem")

    def __init__(self, eng, fn):
        self.eng = eng
        self.fn = fn
        self.deps = []
        self.flag = False
        self.cnt = 0
        self.dma_res = None
        self.dma_val = 0


ENGS = ("pe", "act", "dve", "pool", "sp")
import os as _os
SEM_LIMIT = int(_os.environ.get("SEM_LIMIT", "24000"))


class Sched:
    def __init__(self, nc):
        self.nc = nc
        self.stack = ExitStack()
        self.root = self.stack
        self.ops = {e: [] for e in ENGS}
        self.nres = 0
        self.nsem = 0
        self.free_sems = []
        self.all_res = []
        self.bar_tok = None
        self.scopes = []

    def res(self, name=None):
        self.nres += 1
        r = Res("%s_%d" % (name or "r", self.nres))
        r.last_w = self.bar_tok
        self.all_res.append(r)
        return r

    def push(self):
        self.scopes.append((self.stack, len(self.all_res)))
        self.stack = ExitStack()

    def pop(self):
        self.barrier()
        self.stack.close()
        self.stack, n0 = self.scopes.pop()
        for r in self.all_res[n0:]:
            if r.dma_sem is not None and r.name.startswith("sb_"):
                self.free_sems.append((r.dma_sem, r.dma_cnt))
                r.dma_sem = None
        self.all_res = self.all_res[:n0]

    def barrier(self):
        for e in ENGS:
            o = self.op(e, (lambda eng: eng.nop()), reads=(), writes=list(self.all_res))
        self.bar_tok = ("op", o, None)

    def I(self, eng, name, reads, writes, *args, **kw):
        return self.op(eng, (lambda e: getattr(e, name)(*args, **kw)), reads=reads, writes=writes)

    def D(self, eng, dst, srcs, out, in_, **kw):
        return self.dma(eng, (lambda e: e.dma_start(out=out, in_=in_, **kw)), dst, srcs=srcs)

    def sb(self, name, shape, dtype):
        self.nres += 1
        name = "sb_%s_%d" % (name, self.nres)
        t = self.stack.enter_context(self.nc.sbuf_tensor(name, list(shape), dtype))
        t_res = self.res(name)
        return T(t, t_res)

    def ps(self, name, shape, dtype=F32):
        self.nres += 1
        name = "ps_%s_%d" % (name, self.nres)
        t = self.stack.enter_context(self.nc.psum_tensor(name, list(shape), dtype))
        r = self.res(name)
        r.excl = True
        return T(t, r)

    def dram(self, name, shape, dtype, kind="Internal"):
        t = self.nc.dram_tensor(name, list(shape), dtype, kind=kind)
        return T(t.ap(), self.res(name))

    def _dep_tokens(self, op, reads, writes):
        toks = []
        for r in reads:
            if r.last_w is not None:
                toks.append(r.last_w)
            if r.excl:
                toks.extend(tk for tk in r.readers if tk[0] == "op" and tk[1].eng != op.eng)
        for w in writes:
            if w.last_w is not None:
                toks.append(w.last_w)
            toks.extend(w.readers)
        for tk in toks:
            if tk[0] == "op":
                if tk[1].eng == op.eng and op.eng in ("pe", "sp"):
                    continue
                op.deps.append(tk)
            else:
                op.deps.append(("dma", tk[1], tk[1].dma_snapshot()))

    def op(self, eng, fn, reads=(), writes=()):
        reads = [r.res if isinstance(r, T) else r for r in reads]
        writes = [w.res if isinstance(w, T) else w for w in writes]
        o = Op(eng, fn)
        self._dep_tokens(o, reads, writes)
        tok = ("op", o, None)
        if fn is not None:
            for r in reads:
                r.readers.append(tok)
            for w in writes:
                w.last_w = tok
                w.readers = []
        self.ops[eng].append(o)
        return o

    def dma(self, eng, fn, dst, srcs=(), extra_reads=()):
        dres = dst.res if isinstance(dst, T) else dst
        reads = [r.res if isinstance(r, T) else r for r in list(srcs) + list(extra_reads)]
        o = Op(eng, fn)
        self._dep_tokens(o, reads, [dres])
        if dres.dma_sem is not None and dres.dma_cnt + 16 > SEM_LIMIT:
            dres.old_sems.append((dres.dma_sem, dres.dma_cnt))
            dres.dma_sem = None
        if dres.dma_sem is None:
            while self.free_sems and self.free_sems[-1][1] + 16 > SEM_LIMIT:
                self.free_sems.pop()
            if self.free_sems:
                dres.dma_sem, dres.dma_cnt = self.free_sems.pop()
            else:
                self.nsem += 1
                dres.dma_sem = self.root.enter_context(self.nc.semaphore("d%d_%s" % (self.nsem, dres.name)))
                dres.dma_cnt = 0
        dres.dma_cnt += 16
        o.dma_res = dres
        o.dma_val = dres.dma_cnt
        o.dma_sem = dres.dma_sem
        tok = ("dma", dres, dres.dma_cnt)
        for r in reads:
            r.readers.append(tok)
        dres.last_w = tok
        dres.readers = []
        self.ops[eng].append(o)
        return o

    def fence(self, eng, reads):
        return self.op(eng, None, reads=reads, writes=())

    def emit(self):
        nc = self.nc
        for e in ENGS:
            for o in self.ops[e]:
                for d in o.deps:
                    if d[0] == "op":
                        d[1].flag = True
        for e in ENGS:
            c = 0
            ep = 0
            sem = None
            for o in self.ops[e]:
                if o.flag:
                    if sem is None or c >= SEM_LIMIT:
                        ep += 1
                        sem = self.root.enter_context(nc.semaphore("s_%s_%d" % (e, ep)))
                        c = 0
                    c += 1
                    o.cnt = c
                    o.sem = sem
        ops = self.ops

        def run(e, engine):
            waited = {}
            for o in ops[e]:
                need = {}
                for d in o.deps:
                    if d[0] == "op":
                        pairs = ((d[1].sem, d[1].cnt),)
                    else:
                        pairs = d[2]
                    for (s_, v) in pairs:
                        k = id(s_)
                        if waited.get(k, 0) >= v:
                            continue
                        if k not in need or need[k][1] < v:
                            need[k] = (s_, v)
                for k, (s_, v) in need.items():
                    engine.wait_ge(s_, v)
                    waited[k] = v
                if o.fn is None:
                    continue
                ins = o.fn(engine)
                if o.dma_res is not None:
                    ins.then_inc(o.dma_sem, 16)
                elif o.flag:
                    ins.then_inc(o.sem, 1)

        with nc.Block() as block:
            @block.tensor
            def _(eng):
                run("pe", eng)

            @block.scalar
            def _(eng):
                run("act", eng)

            @block.vector
            def _(eng):
                run("dve", eng)

            @block.gpsimd
            def _(eng):
                run("pool", eng)

            @block.sync
            def _(eng):
                run("sp", eng)

    def close(self):
        self.stack.close()


class T:
    __slots__ = ("t", "res")

    def __init__(self, t, res):
        self.t = t
        self.res = res

    def __getitem__(self, k):
        return self.t[k]


import math

SEQ = 2048
DM = 1024
MEM = 256
TW = 3968
TOFF = 1920


class Dd:
    pass


def declare(S, NSEQ, dbg=False, big=True, depth=1):
    d = Dd()
    NT = NSEQ * SEQ
    ein = lambda n, shp, dt=F32: S.dram(n, [depth] + shp, dt, kind="ExternalInput")
    d.depth = depth
    d.x_in = S.dram("x", [NT, DM], F32, kind="ExternalInput")
    d.mem = S.dram("mem", [NSEQ * MEM, DM], F32, kind="ExternalInput")
    d.st = {}
    d.st["norm_mix"] = ein("norm_mix", [1, DM])
    d.st["w_in"] = ein("w_in", [DM, 3584])
    d.st["decay"] = ein("decay", [1, 16])
    d.st["ret_gain"] = ein("ret_gain", [64, 8])
    d.st["att_gain"] = ein("att_gain", [64, 8])
    d.st["w_out"] = ein("w_out", [DM, DM])
    d.st["norm_mem"] = ein("norm_mem", [1, DM])
    d.st["norm_mem_kv"] = ein("norm_mem_kv", [1, DM])
    d.st["w_mem_q"] = ein("w_mem_q", [DM, DM])
    d.st["w_mem_kv"] = ein("w_mem_kv", [DM, 2 * DM])
    d.st["w_mem_o"] = ein("w_mem_o", [DM, DM])
    d.st["norm_ffn"] = ein("norm_ffn", [1, DM])
    d.st["peer_wq"] = ein("peer_wq", [DM, 2048])
    d.st["sub_keys"] = ein("sub_keys", [256, 128])
    if big:
        d.st["e_down"] = ein("e_down", [16384, DM])
        d.st["e_up"] = ein("e_up", [16384, DM])
    d.norm_final = S.dram("norm_final", [1, DM], F32, kind="ExternalInput")
    d.xn = S.dram("xn", [NT, DM], F32, kind="ExternalOutput")
    kd = "ExternalOutput" if dbg else "Internal"
    d.xbuf = [S.dram("xbuf%d" % i, [NT, DM], F32, kind=("ExternalOutput" if (dbg and i == 0) else "Internal")) for i in range(2)]
    d.projT = [S.dram("projT%d" % s, [2560, SEQ], BF16, kind=kd) for s in range(NSEQ)]
    d.vd = S.dram("vd", [NT, DM], BF16, kind=kd)
    d.cT = S.dram("cT", [DM, NT], BF16, kind=kd)
    d.x2 = S.dram("x2", [NT, DM], F32, kind=kd)
    if big:
        d.ed_bf = S.dram("ed_bf", [16384, DM], BF16)
        d.eu_bf = S.dram("eu_bf", [16384, DM], BF16)
    d.kt_d = S.dram("kt_d", [NSEQ, 128, 8, MEM], BF16)
    d.vm_d = S.dram("vm_d", [NSEQ, 128, 2, DM], BF16)
    return d


def layer_view(d, L):
    v = Dd()
    v.__dict__.update(d.__dict__)
    for k, t in d.st.items():
        setattr(v, k, T(t.t[L], t.res))
    last = (L == d.depth - 1)
    if "e_down" in d.st:
        v.e_down_all = T(d.st["e_down"].t.rearrange("l e c -> (l e) c"), d.st["e_down"].res)
        v.e_up_all = T(d.st["e_up"].t.rearrange("l e c -> (l e) c"), d.st["e_up"].res)
    v.ebase = float(L * 16384)
    v.x = d.x_in if L == 0 else d.xbuf[(L - 1) % 2]
    v.xo = d.xbuf[L % 2]
    v.last = last
    return v


def make_ident(S):
    ident = S.sb("ident", [128, 128], BF16)
    io_i = S.sb("idio_i", [128, 128], I32)
    io_f = S.sb("idio_f", [128, 128], F32)
    S.I("pool", "iota", [], [io_i], io_i[:], pattern=[[1, 128]], base=0, channel_multiplier=-1)
    S.I("dve", "tensor_copy", [io_i], [io_f], out=io_f[:], in_=io_i[:])
    S.I("dve", "tensor_scalar", [io_f], [ident], out=ident[:], in0=io_f[:], scalar1=0.0, scalar2=None, op0=ALU.is_equal)
    return ident


def load_w_bf16(S, name, src, kc, ncol, p=128):
    w = S.sb(name, [p, kc, ncol], BF16)
    v = src.t.rearrange("(k p) n -> p k n", p=p)
    for c0 in range(0, ncol, 512):
        c1 = min(ncol, c0 + 512)
        S.D("pool", w, [src], w[:, :, c0:c1], v[:, :, c0:c1])
    return w


def rms_tile(S, xin, xres, nsub, gb, hout, ssq, rs, junk):
    S.I("dve", "memset", [], [ssq], ssq[:], 0.0)
    for j in range(nsub):
        S.I("act", "activation", [xres], [junk, ssq], out=junk[:], in_=xin[:, j, :], func=AF.Square, accum_out=ssq[:, j:j + 1])
    S.I("act", "activation", [ssq], [rs], out=rs[:, 0:nsub], in_=ssq[:, 0:nsub], func=AF.Sqrt, bias=1e-6, scale=1.0 / DM)
    S.I("dve", "reciprocal", [rs], [rs], out=rs[:, 0:nsub], in_=rs[:, 0:nsub])
    for j in range(nsub):
        for ho in hout:
            S.I("dve", "scalar_tensor_tensor", [xres, rs, gb], [ho], out=ho[:, j, :], in0=xin[:, j, :], scalar=rs[:, j:j + 1], in1=gb[:], op0=ALU.mult, op1=ALU.mult)


def phase1(S, d, NSEQ, ident):
    S.push()
    w = load_w_bf16(S, "w_in", d.w_in, 8, 3584)
    gb = S.sb("gb1", [128, DM], F32)
    S.D("sp", gb, [d.norm_mix], gb[:], d.norm_mix[0:1, :].partition_broadcast(128))
    xs = [S.sb("xs%d" % i, [128, 4, DM], F32) for i in range(2)]
    junk = S.sb("junk1", [128, DM], BF16)
    ssq = [S.sb("ssq%d" % i, [128, 4], F32) for i in range(2)]
    rs = [S.sb("rs%d" % i, [128, 4], F32) for i in range(2)]
    h = S.sb("h1", [128, 4, DM], BF16)
    hT = [S.sb("hT%d" % i, [128, 8, 512], BF16) for i in range(2)]
    fo = [S.sb("fo%d" % i, [128, 512], BF16) for i in range(4)]
    pt = [S.ps("pt%d" % i, [128, 8, 128], BF16) for i in range(2)]
    pf = [S.ps("pf%d" % i, [128, 512], F32) for i in range(4)]
    fm = []
    for c in range(4):
        fm.append((c * 128, c * 128, False))
    for c in range(4):
        fm.append((512 + c * 128, 512 + c * 128, False))
    for c in range(4):
        fm.append((1536 + c * 128, 1024 + c * 128, True))
    for c in range(4):
        fm.append((2048 + c * 128, 1536 + c * 128, False))
    for c in range(4):
        fm.append((2560 + c * 128, 2048 + c * 128, False))
    nfo = 0
    npf = 0
    for s in range(NSEQ):
        for tt in range(4):
            i = s * 4 + tt
            x_ = xs[i % 2]
            hT_ = hT[i % 2]
            r0 = s * SEQ + tt * 512
            S.D("sp", x_, [d.x], x_[:], d.x.t[r0:r0 + 512, :].rearrange("(j p) c -> p j c", p=128))
            rms_tile(S, x_, x_, 4, gb, [h], ssq[i % 2], rs[i % 2], junk)
            for j in range(4):
                p_ = pt[j % 2]
                for k in range(8):
                    S.I("pe", "transpose", [h, ident], [p_], out=p_[:, k, :], in_=h[:, j, k * 128:(k + 1) * 128], identity=ident[:])
                if j % 2 == 0:
                    S.I("act", "copy", [p_], [hT_], out=hT_[:, :, j * 128:(j + 1) * 128], in_=p_[:])
                else:
                    S.I("dve", "tensor_copy", [p_], [hT_], out=hT_[:, :, j * 128:(j + 1) * 128], in_=p_[:])
            for (col, row, silu) in fm:
                p_ = pf[npf % 4]
                npf += 1
                f_ = fo[nfo % 4]
                nfo += 1
                for k in range(8):
                    S.I("pe", "matmul", [w, hT_], [p_], p_[:], lhsT=w[:, k, col:col + 128], rhs=hT_[:, k, :], start=(k == 0), stop=(k == 7))
                if silu:
                    S.I("act", "activation", [p_], [f_], out=f_[:], in_=p_[:], func=AF.Silu)
                elif nfo % 2 == 0:
                    S.I("act", "copy", [p_], [f_], out=f_[:], in_=p_[:])
                else:
                    S.I("dve", "tensor_copy", [p_], [f_], out=f_[:], in_=p_[:])
                S.D("sp", d.projT[s], [f_], d.projT[s][row:row + 128, tt * 512:(tt + 1) * 512], f_[:])
            for j in range(4):
                for vi, col in enumerate((1024, 3072)):
                    p_ = pf[npf % 4]
                    npf += 1
                    f_ = fo[nfo % 4]
                    nfo += 1
                    for k in range(8):
                        S.I("pe", "matmul", [w, hT_], [p_], p_[:], lhsT=hT_[:, k, j * 128:(j + 1) * 128], rhs=w[:, k, col:col + 512], start=(k == 0), stop=(k == 7))
                    if nfo % 2 == 0:
                        S.I("act", "copy", [p_], [f_], out=f_[:], in_=p_[:])
                    else:
                        S.I("dve", "tensor_copy", [p_], [f_], out=f_[:], in_=p_[:])
                    S.D("sp", d.vd, [f_], d.vd[r0 + j * 128:r0 + (j + 1) * 128, vi * 512:(vi + 1) * 512], f_[:])
    S.pop()


def phase2(S, d, NSEQ, heads=range(16), tts=range(4)):
    S.push()
    dl_i = S.sb("dl_i", [128, TW], I32)
    A = S.sb("tA", [128, TW], F32)
    B = S.sb("tB", [128, TW], F32)
    rpos = S.sb("rpos", [128, TW], F32)
    rneg = S.sb("rneg", [128, TW], F32)
    multb = S.sb("multb", [128, TW], BF16)
    tab = [S.sb("tab%d" % i, [128, TW], BF16) for i in range(2)]
    S.I("pool", "iota", [], [dl_i], dl_i[:], pattern=[[1, TW]], base=-TOFF, channel_multiplier=-1)
    S.I("dve", "tensor_copy", [dl_i], [A], out=A[:], in_=dl_i[:])
    S.I("dve", "tensor_scalar", [A], [rpos], out=rpos[:], in0=A[:], scalar1=0.0, scalar2=None, op0=ALU.max)
    S.I("dve", "tensor_scalar", [A], [rneg], out=rneg[:], in0=A[:], scalar1=-1.0, scalar2=0.0, op0=ALU.mult, op1=ALU.max)
    S.I("dve", "tensor_tensor", [rpos, rneg], [A], out=A[:], in0=rpos[:], in1=rneg[:], op=ALU.add)
    S.I("dve", "tensor_scalar", [A], [multb], out=multb[:], in0=A[:], scalar1=64.0, scalar2=None, op0=ALU.is_le)
    tmp_i = dl_i
    for (inv, lim) in ((0.25, 256.0), (0.0625, 1024.0)):
        S.I("dve", "tensor_scalar", [A], [B], out=B[:], in0=A[:], scalar1=inv, scalar2=None, op0=ALU.mult)
        S.I("dve", "tensor_copy", [B], [tmp_i], out=tmp_i[:], in_=B[:])
        S.I("dve", "tensor_tensor", [B, tmp_i], [B], out=B[:], in0=B[:], in1=tmp_i[:], op=ALU.is_equal)
        S.I("dve", "scalar_tensor_tensor", [A, B], [B], out=B[:], in0=A[:], scalar=lim, in1=B[:], op0=ALU.is_le, op1=ALU.mult)
        S.I("dve", "tensor_tensor", [multb, B], [multb], out=multb[:], in0=multb[:], in1=B[:], op=ALU.add)
    lg = S.sb("lg", [128, 16], F32)
    S.D("sp", lg, [d.decay], lg[:], d.decay[0:1, :].partition_broadcast(128))
    S.I("act", "activation", [lg], [lg], out=lg[:], in_=lg[:], func=AF.Exp, scale=-1.0)
    S.I("act", "activation", [lg], [lg], out=lg[:], in_=lg[:], func=AF.Ln, bias=1.0)
    S.I("dve", "tensor_scalar", [lg], [lg], out=lg[:], in0=lg[:], scalar1=-1.0, scalar2=None, op0=ALU.mult)
    rgain = S.sb("rgain", [64, 8], F32)
    again = S.sb("again", [64, 8], F32)
    S.D("sp", rgain, [d.ret_gain], rgain[:], d.ret_gain[:, :])
    S.D("sp", again, [d.att_gain], again[:], d.att_gain[:, :])
    ones64 = S.sb("ones64", [64, 64], BF16)
    S.I("dve", "memset", [], [ones64], ones64[:], 1.0)
    sel = S.sb("sel", [128, 64], BF16)
    S.I("dve", "memset", [], [sel], sel[:], 0.0)
    S.I("dve", "memset", [sel], [sel], sel[64:65, :], 1.0)
    qT = [S.sb("qT%d" % i, [64, SEQ], BF16) for i in range(2)]
    kT = [S.sb("kT%d" % i, [64, SEQ], BF16) for i in range(2)]
    gT = [S.sb("gT%d" % i, [64, SEQ], BF16) for i in range(2)]
    V = [S.sb("V%d" % i, [128, 16, 128], BF16) for i in range(2)]
    for i in range(2):
        S.I("dve", "memset", [], [V[i]], V[i][:], 0.0)
        S.I("dve", "memset", [V[i]], [V[i]], V[i][:, :, 64:65], 1.0)
    PT = [S.sb("PT%d" % i, [128, 512], BF16) for i in range(3)]
    E = [S.sb("E%d" % i, [128, 512], BF16) for i in range(2)]
    Osb = S.sb("Osb", [64, 512], F32)
    Ob = S.sb("Ob", [128, 512], BF16)
    S.I("dve", "memset", [], [Ob], Ob[:], 0.0)
    sq = S.sb("sq", [64, 512], BF16)
    t1 = S.sb("t1", [64, 512], F32)
    rstd = S.sb("rstd", [64, 512], F32)
    yb = S.sb("yb", [64, 512], F32)
    cTs = [S.sb("cTs%d" % i, [64, 512], BF16) for i in range(2)]
    psc = [S.ps("psc%d" % i, [128, 512], F32) for i in range(3)]
    po = [S.ps("po%d" % i, [128, 512], F32) for i in range(2)]
    pss = S.ps("pss", [64, 512], F32)
    pl = S.ps("pl", [64, 512], F32)
    LN8 = math.log(0.125)
    nb = 0
    nt = 0
    for s in range(NSEQ):
        for hh in heads:
            sl = (s * 16 + hh) % 2
            isret = hh < 8
            h = hh % 8
            if isret:
                qr, kr, gr, vc = h * 64, 512 + h * 64, 1024 + h * 64, h * 64
            else:
                qr, kr, gr, vc = 1536 + h * 64, 2048 + h * 64, None, 512 + h * 64
            pj = d.projT[s]
            S.D("sp", qT[sl], [pj], qT[sl][:], pj[qr:qr + 64, :])
            S.D("sp", kT[sl], [pj], kT[sl][:], pj[kr:kr + 64, :])
            if isret:
                S.D("sp", gT[sl], [pj], gT[sl][:], pj[gr:gr + 64, :])
            for c4 in range(4):
                S.D("sp", V[sl], [d.vd], V[sl][:, c4 * 4:(c4 + 1) * 4, 0:64], d.vd[s * SEQ + c4 * 512:s * SEQ + (c4 + 1) * 512, vc:vc + 64].rearrange("(c p) e -> p c e", p=128))
            tb = tab[sl]
            if isret:
                S.I("dve", "tensor_scalar", [rpos, lg], [B], out=B[:], in0=rpos[:], scalar1=lg[:, h:h + 1], scalar2=None, op0=ALU.mult)
                S.I("dve", "scalar_tensor_tensor", [rneg, lg, B], [B], out=B[:], in0=rneg[:], scalar=lg[:, 8 + h:9 + h], in1=B[:], op0=ALU.mult, op1=ALU.add)
                S.I("act", "activation", [B], [tb], out=tb[:], in_=B[:], func=AF.Exp, bias=LN8)
            else:
                slope = 2.0 ** (-(h + 1))
                S.I("act", "activation", [A], [tb], out=tb[:], in_=A[:], func=AF.Exp, scale=-slope)
                S.I("dve", "tensor_tensor", [tb, multb], [tb], out=tb[:], in0=tb[:], in1=multb[:], op=ALU.mult)
            M = 64 if isret else 128
            for tt in tts:
                T0 = tt * 512
                blocks = []
                for c in range(16):
                    S0 = c * 128
                    if (not isret) and (T0 - S0 - 127 > 1024 or T0 - S0 + 511 < -1024):
                        continue
                    blocks.append(c)
                po_ = po[nt % 2]
                nt += 1
                for bi, c in enumerate(blocks):
                    S0 = c * 128
                    u0 = T0 - S0 + TOFF
                    sc = psc[nb % 3]
                    pt_ = PT[nb % 3]
                    e_ = E[nb % 2]
                    nb += 1
                    S.I("pe", "matmul", [kT[sl], qT[sl]], [sc], sc[:], lhsT=kT[sl][:, S0:S0 + 128], rhs=qT[sl][:, T0:T0 + 512], start=True, stop=True)
                    if isret:
                        S.I("dve", "tensor_tensor", [sc, tb], [pt_], out=pt_[:], in0=sc[:], in1=tb[:, u0:u0 + 512], op=ALU.mult)
                    else:
                        S.I("act", "activation", [sc], [e_], out=e_[:], in_=sc[:], func=AF.Exp, scale=0.125)
                        S.I("dve", "tensor_tensor", [e_, tb], [pt_], out=pt_[:], in0=e_[:], in1=tb[:, u0:u0 + 512], op=ALU.mult)
                    S.I("pe", "matmul", [V[sl], pt_], [po_], po_[0:M, :], lhsT=V[sl][:, c, 0:M], rhs=pt_[:], start=(bi == 0), stop=(bi == len(blocks) - 1))
                import os as _os
                if _os.environ.get("NOEPI"):
                    continue
                gcol = rgain[:, h:h + 1] if isret else again[:, h:h + 1]
                gres = rgain if isret else again
                ct_ = cTs[nt % 2]
                S.I("act", "copy", [po_], [Osb], out=Osb[:], in_=po_[0:64, :])
                S.I("act", "activation", [po_], [sq], out=sq[:], in_=po_[0:64, :], func=AF.Square)
                S.I("pe", "matmul", [ones64, sq], [pss], pss[:], lhsT=ones64[:], rhs=sq[:], start=True, stop=True)
                if isret:
                    S.I("act", "activation", [pss], [rstd], out=rstd[:], in_=pss[:], func=AF.Sqrt, bias=1e-6, scale=1.0 / 64)
                else:
                    lvl = int(_os.environ.get("EPI", "9"))
                    if lvl >= 0:
                        S.I("act", "copy", [po_], [Ob], out=Ob[:], in_=po_[:])
                    if lvl >= 1:
                        S.I("pe", "matmul", [sel, Ob], [pl], pl[:], lhsT=sel[:], rhs=Ob[:], start=True, stop=True)
                    if lvl >= 2:
                        S.I("act", "activation", [pl], [t1], out=t1[:], in_=pl[:], func=AF.Square, scale=1e-3)
                    if lvl >= 3:
                        S.I("dve", "scalar_tensor_tensor", [pss, t1], [t1], out=t1[:], in0=pss[:], scalar=1.0 / 64, in1=t1[:], op0=ALU.mult, op1=ALU.add)
                    if lvl >= 4:
                        S.I("act", "activation", [t1], [rstd], out=rstd[:], in_=t1[:], func=AF.Sqrt)
                    if lvl < 9:
                        continue
                S.I("dve", "reciprocal", [rstd], [rstd], out=rstd[:], in_=rstd[:])
                if isret:
                    S.I("dve", "scalar_tensor_tensor", [Osb, gres, rstd], [yb], out=yb[:], in0=Osb[0:64, :], scalar=gcol, in1=rstd[:], op0=ALU.mult, op1=ALU.mult)
                    S.I("dve", "tensor_tensor", [yb, gT[sl]], [ct_], out=ct_[:], in0=yb[:], in1=gT[sl][:, T0:T0 + 512], op=ALU.mult)
                else:
                    S.I("dve", "scalar_tensor_tensor", [Osb, gres, rstd], [ct_], out=ct_[:], in0=Osb[0:64, :], scalar=gcol, in1=rstd[:], op0=ALU.mult, op1=ALU.mult)
                S.D("sp", d.cT, [ct_], d.cT[hh * 64:(hh + 1) * 64, s * SEQ + T0:s * SEQ + T0 + 512], ct_[:])
    S.pop()


def phase0(S, d):
    S.push()
    buf = [S.sb("cv%d" % i, [128, 16, DM], BF16) for i in range(2)]
    n = 0
    for src, dst in ((d.e_down, d.ed_bf), (d.e_up, d.eu_bf)):
        for ch in range(8):
            b_ = buf[n % 2]
            n += 1
            sv = src.t[ch * 2048:(ch + 1) * 2048, :].rearrange("(p j) c -> p j c", p=128)
            dv = dst.t[ch * 2048:(ch + 1) * 2048, :].rearrange("(p j) c -> p j c", p=128)
            for q in range(4):
                S.D("pool", b_, [src], b_[:, q * 4:(q + 1) * 4, :], sv[:, q * 4:(q + 1) * 4, :])
            S.D("sp", dst, [b_], dv, b_[:])
    S.pop()


def phase3a(S, d, NSEQ, ident):
    S.push()
    w_kv = load_w_bf16(S, "w_mkv", d.w_mem_kv, 8, 2 * DM)
    gkv = S.sb("g_kv", [128, DM], F32)
    S.D("sp", gkv, [d.norm_mem_kv], gkv[:], d.norm_mem_kv[0:1, :].partition_broadcast(128))
    junk = S.sb("junk3a", [128, DM], F32)
    ssq = S.sb("ssq3a", [128, 4], F32)
    rs = S.sb("rs3a", [128, 4], F32)
    mx = S.sb("mx", [128, 2, DM], F32)
    mh = S.sb("mh", [128, 2, DM], BF16)
    mT = S.sb("mT", [128, 8, MEM], BF16)
    KT = S.sb("KTs", [128, 8, MEM], BF16)
    VM = S.sb("VMs", [128, 2, DM], BF16)
    p_t = S.ps("p_t3a", [128, 8, 128], BF16)
    p_a = S.ps("p_a3a", [128, MEM], F32)
    p_y = S.ps("p_y3a", [128, DM], F32)
    for s in range(NSEQ):
        S.D("sp", mx, [d.mem], mx[:], d.mem.t[s * MEM:(s + 1) * MEM, :].rearrange("(j p) c -> p j c", p=128))
        rms_tile(S, mx, mx, 2, gkv, [mh], ssq, rs, junk)
        for j in range(2):
            for k in range(8):
                S.I("pe", "transpose", [mh, ident], [p_t], out=p_t[:, k, :], in_=mh[:, j, k * 128:(k + 1) * 128], identity=ident[:])
            S.I("act", "copy", [p_t], [mT], out=mT[:, :, j * 128:(j + 1) * 128], in_=p_t[:])
        for f in range(8):
            for k in range(8):
                S.I("pe", "matmul", [w_kv, mT], [p_a], p_a[:], lhsT=w_kv[:, k, f * 128:(f + 1) * 128], rhs=mT[:, k, :], start=(k == 0), stop=(k == 7))
            S.I("act", "copy", [p_a], [KT], out=KT[:, f, :], in_=p_a[:])
        for j in range(2):
            for hf in range(2):
                for k in range(8):
                    S.I("pe", "matmul", [w_kv, mT], [p_y], p_y[:, hf * 512:(hf + 1) * 512], lhsT=mT[:, k, j * 128:(j + 1) * 128], rhs=w_kv[:, k, DM + hf * 512:DM + (hf + 1) * 512], start=(k == 0), stop=(k == 7))
            S.I("dve", "tensor_copy", [p_y], [VM], out=VM[:, j, :], in_=p_y[:])
        S.D("sp", d.kt_d, [KT], d.kt_d[s], KT[:])
        S.D("sp", d.vm_d, [VM], d.vm_d[s], VM[:])
    S.pop()


def phase3b(S, d, NSEQ, ident, ntiles=None):
    S.push()
    NT = NSEQ * SEQ
    w_out = load_w_bf16(S, "w_out", d.w_out, 16, DM, p=64)
    w_q = load_w_bf16(S, "w_mq", d.w_mem_q, 8, DM)
    w_o = load_w_bf16(S, "w_mo", d.w_mem_o, 8, DM)
    gm = S.sb("g_mem", [128, DM], F32)
    S.D("sp", gm, [d.norm_mem], gm[:], d.norm_mem[0:1, :].partition_broadcast(128))
    junk = S.sb("junk3b", [128, DM], F32)
    ssq = S.sb("ssq3b", [128, 4], F32)
    rs = S.sb("rs3b", [128, 4], F32)
    p_y = [S.ps("p_y%d" % i, [128, DM], F32) for i in range(2)]
    p_t = S.ps("p_t", [128, 8, 128], BF16)
    p_a = S.ps("p_a", [128, 8, 128], F32)
    p_l = S.ps("p_l", [128, 4, 128], F32)
    ones = S.sb("ones3", [128, 128], BF16)
    S.I("dve", "memset", [], [ones], ones[:], 1.0)
    KT = S.sb("KT", [128, 8, MEM], BF16)
    VM = S.sb("VM", [128, 2, DM], BF16)
    xt = [S.sb("xt%d" % i, [128, 1, DM], F32) for i in range(2)]
    cTt = [S.sb("cTt%d" % i, [64, 16, 128], BF16) for i in range(2)]
    hb = S.sb("hb", [128, 1, DM], BF16)
    hT = S.sb("hT3", [128, 8, 128], BF16)
    qT = S.sb("qT3", [128, 8, 128], BF16)
    PTm = S.sb("PTm", [128, 8, 128], BF16)
    rL = S.sb("rL", [128, 4, 128], F32)
    oT = S.sb("oT", [128, 8, 128], BF16)
    ntl = NT // 128 if ntiles is None else ntiles
    for ti in range(ntl):
        s = ti // 16
        r0 = ti * 128
        ct_ = cTt[ti % 2]
        x_ = xt[ti % 2]
        if ti % 16 == 0:
            S.D("sp", KT, [d.kt_d], KT[:], d.kt_d[s])
            S.D("sp", VM, [d.vm_d], VM[:], d.vm_d[s])
        S.D("sp", x_, [d.x], x_[:, 0, :], d.x[r0:r0 + 128, :])
        for q4 in range(4):
            S.D("sp", ct_, [d.cT], ct_[:, q4 * 4:(q4 + 1) * 4, :], d.cT.t[q4 * 256:(q4 + 1) * 256, r0:r0 + 128].rearrange("(h e) t -> e h t", e=64))
        py = p_y[0]
        for hf in range(2):
            for hh in range(16):
                S.I("pe", "matmul", [ct_, w_out], [py], py[:, hf * 512:(hf + 1) * 512], lhsT=ct_[:, hh, :], rhs=w_out[:, hh, hf * 512:(hf + 1) * 512], start=(hh == 0), stop=(hh == 15))
        S.I("dve", "tensor_tensor", [py, x_], [x_], out=x_[:, 0, :], in0=py[:], in1=x_[:, 0, :], op=ALU.add)
        rms_tile(S, x_, x_, 1, gm, [hb], ssq, rs, junk)
        for k in range(8):
            S.I("pe", "transpose", [hb, ident], [p_t], out=p_t[:, k, :], in_=hb[:, 0, k * 128:(k + 1) * 128], identity=ident[:])
        S.I("act", "copy", [p_t], [hT], out=hT[:], in_=p_t[:])
        for f in range(8):
            for k in range(8):
                S.I("pe", "matmul", [w_q, hT], [p_a], p_a[:, f, :], lhsT=w_q[:, k, f * 128:(f + 1) * 128], rhs=hT[:, k, :], start=(k == 0), stop=(k == 7))
        S.I("act", "copy", [p_a], [qT], out=qT[:], in_=p_a[:])
        pb = p_y[1]
        pbv = pb[:].rearrange("p (a b) -> p a b", b=128)
        for h in range(4):
            for mc in range(2):
                for dc in range(2):
                    S.I("pe", "matmul", [KT, qT], [pb], pbv[:, h * 2 + mc, :], lhsT=KT[:, 2 * h + dc, mc * 128:(mc + 1) * 128], rhs=qT[:, 2 * h + dc, :], start=(dc == 0), stop=(dc == 1))
        S.I("act", "activation", [pb], [PTm], out=PTm[:], in_=pbv, func=AF.Exp, scale=1.0 / 16)
        for h in range(4):
            for dc in range(2):
                for mc in range(2):
                    S.I("pe", "matmul", [VM, PTm], [p_a], p_a[:, 2 * h + dc, :], lhsT=VM[:, mc, (2 * h + dc) * 128:(2 * h + dc + 1) * 128], rhs=PTm[:, h * 2 + mc, :], start=(mc == 0), stop=(mc == 1))
            for mc in range(2):
                S.I("pe", "matmul", [ones, PTm], [p_l], p_l[:, h, :], lhsT=ones[:], rhs=PTm[:, h * 2 + mc, :], start=(mc == 0), stop=(mc == 1))
        S.I("dve", "reciprocal", [p_l], [rL], out=rL[:], in_=p_l[:])
        for h in range(4):
            S.I("dve", "tensor_tensor", [p_a, rL], [oT], out=oT[:, 2 * h:2 * h + 2, :], in0=p_a[:, 2 * h:2 * h + 2, :], in1=rL[:, h:h + 1, :].to_broadcast([128, 2, 128]), op=ALU.mult)
        for hf in range(2):
            for f in range(8):
                S.I("pe", "matmul", [oT, w_o], [py], py[:, hf * 512:(hf + 1) * 512], lhsT=oT[:, f, :], rhs=w_o[:, f, hf * 512:(hf + 1) * 512], start=(f == 0), stop=(f == 7))
        S.I("dve", "tensor_tensor", [py, x_], [x_], out=x_[:, 0, :], in0=py[:], in1=x_[:, 0, :], op=ALU.add)
        S.D("sp", d.x2, [x_], d.x2[r0:r0 + 128, :], x_[:, 0, :])
    S.pop()


def phase3c(S, d, NSEQ, ident, ntiles=None):
    S.push()
    NT = NSEQ * SEQ
    import os
    SK = os.environ.get("SK", "")
    if "w" not in SK:
        w_pq = load_w_bf16(S, "w_pq", d.peer_wq, 8, 2048)
    gf = S.sb("g_ffn", [128, DM], F32)
    gn = S.sb("g_fin", [128, DM], F32)
    if "g" not in SK:
        S.D("sp", gf, [d.norm_ffn], gf[:], d.norm_ffn[0:1, :].partition_broadcast(128))
        S.D("sp", gn, [d.norm_final], gn[:], d.norm_final[0:1, :].partition_broadcast(128))
    junk = S.sb("junk3c", [128, DM], F32)
    ssq = S.sb("ssq3c", [128, 4], F32)
    rs = S.sb("rs3c", [128, 4], F32)
    p_t = S.ps("p_tc", [128, 8, 128], BF16)
    p_a = S.ps("p_ac", [128, 8, 128], F32)
    p_b = S.ps("p_bc", [128, 8, 128], F32)
    keysT = S.sb("keysT", [128, 2, 128], BF16)
    kf = S.sb("kf", [128, 2, 128], F32)
    kb = S.sb("kb", [128, 2, 128], BF16)
    import os
    SK = os.environ.get("SK", "")
    if "k" not in SK:
        S.D("sp", kf, [d.sub_keys], kf[:], d.sub_keys.t.rearrange("(c k) e -> k c e", k=128))
        S.I("dve", "tensor_copy", [kf], [kb], out=kb[:], in_=kf[:])
        for c in range(2):
            S.I("pe", "transpose", [kb, ident], [p_t], out=p_t[:, c, :], in_=kb[:, c, :], identity=ident[:])
        S.I("act", "copy", [p_t], [keysT], out=keysT[:], in_=p_t[:, 0:2, :])
    io4_i = S.sb("io4_i", [128, 16], I32)
    io4s = S.sb("io4", [128, 16, 16], F32)
    lo16s = S.sb("lo16", [128, 16, 16], F32)
    hi16s = S.sb("hi16", [128, 16, 16], F32)
    if "i" not in SK:
        S.I("pool", "iota", [], [io4_i], io4_i[:], pattern=[[1, 16]], base=0, channel_multiplier=0)
        S.I("dve", "tensor_copy", [io4_i], [io4s], out=io4s[:], in_=io4_i[:].unsqueeze(1).to_broadcast([128, 16, 16]))
        S.I("dve", "tensor_scalar", [io4s], [lo16s], out=lo16s[:], in0=io4s[:], scalar1=16.0, scalar2=None, op0=ALU.mult)
        S.I("dve", "tensor_scalar", [io4s], [hi16s], out=hi16s[:], in0=io4s[:], scalar1=16.0, scalar2=16.0, op0=ALU.mult, op1=ALU.add)
    B4 = [128, 8, 16, 16]
    io4 = T(io4s[:].unsqueeze(1).to_broadcast(B4), io4s.res)
    lo16 = T(lo16s[:].unsqueeze(1).to_broadcast(B4), lo16s.res)
    hi16 = T(hi16s[:].unsqueeze(1).to_broadcast(B4), hi16s.res)

    xt = [S.sb("xc%d" % i, [128, 1, DM], F32) for i in range(2)]
    xnb = S.sb("xnb", [128, 1, DM], F32)
    hb = S.sb("hbc", [128, 1, DM], BF16)
    hf3 = S.sb("hf3", [128, 1, DM], F32)
    hT = S.sb("hTc", [128, 8, 128], BF16)
    qpT = S.sb("qpT", [128, 16, 128], BF16)
    Ssb = S.sb("Ssb", [128, 16, 128], F32)
    S2 = [S.sb("S2_%d" % i, [128, 128], F32) for i in range(2)]
    Tv = S.sb("Tv", [128, 16, 16], F32)
    Ti = S.sb("Ti", [128, 16, 16], U32)
    Tif = S.sb("Tif", [128, 16, 16], F32)
    cand = S.sb("cand", [128, 8, 16, 16], F32)
    cand2 = [S.sb("cand2_%d" % i, [128, 256], F32) for i in range(2)]
    Bv = S.sb("Bv", [128, 8, 16], F32)
    Bp = S.sb("Bp", [128, 8, 16], U32)
    Af = S.sb("Af", [128, 8, 16], F32)
    eq = S.sb("eq", [128, 8, 16, 16], F32)
    eq2 = S.sb("eq2", [128, 8, 16, 16], F32)
    Akf = S.sb("Akf", [128, 8, 16], F32)
    Bf = S.sb("Bf", [128, 8, 16], F32)
    Ik = S.sb("Ik", [128, 8, 16], F32)
    Jk = S.sb("Jk", [128, 8, 16], F32)
    Ef = S.sb("Ef", [128, 128], F32)
    Ei = S.sb("Ei", [128, 128], U32)
    Gt = S.sb("Gt", [128, 8, 16], F32)
    sm = S.sb("sm", [128, 8], F32)
    av = S.sb("av", [128, 128], F32)
    g1 = S.sb("g1", [128, 128], F32)
    g2 = S.sb("g2", [128, 128], F32)
    Wt = S.sb("Wt", [128, 128], F32)
    NROW = 12
    rows = [S.sb("row%d" % i, [128, DM], BF16) for i in range(NROW)]
    nrow = 0
    import os
    CUT = int(os.environ.get("CUT", "99"))
    ntl = NT // 128 if ntiles is None else ntiles
    for ti in range(ntl):
        r0 = ti * 128
        x2 = xt[ti % 2]
        if CUT < 1:
            continue
        S.D("sp", x2, [d.x2], x2[:, 0, :], d.x2[r0:r0 + 128, :])
        rms_tile(S, x2, x2, 1, gf, [hb, hf3], ssq, rs, junk)
        for k in range(8):
            S.I("pe", "transpose", [hb, ident], [p_t], out=p_t[:, k, :], in_=hb[:, 0, k * 128:(k + 1) * 128], identity=ident[:])
        S.I("act", "copy", [p_t], [hT], out=hT[:], in_=p_t[:])
        for half, pp in enumerate((p_a, p_b)):
            for f in range(8):
                fc = half * 8 + f
                for k in range(8):
                    S.I("pe", "matmul", [w_pq, hT], [pp], pp[:, f, :], lhsT=w_pq[:, k, fc * 128:(fc + 1) * 128], rhs=hT[:, k, :], start=(k == 0), stop=(k == 7))
            if half == 0:
                S.I("act", "copy", [pp], [qpT], out=qpT[:, 0:8, :], in_=pp[:])
            else:
                S.I("dve", "tensor_copy", [pp], [qpT], out=qpT[:, 8:16, :], in_=pp[:])
        for half, pp in enumerate((p_a, p_b)):
            for f in range(8):
                hc = half * 8 + f
                S.I("pe", "matmul", [qpT, keysT], [pp], pp[:, f, :], lhsT=qpT[:, hc, :], rhs=keysT[:, hc % 2, :], start=True, stop=True)
            if half == 0:
                S.I("act", "copy", [pp], [Ssb], out=Ssb[:, 0:8, :], in_=pp[:])
            else:
                S.I("dve", "tensor_copy", [pp], [Ssb], out=Ssb[:, 8:16, :], in_=pp[:])
        if CUT < 2:
            continue
        for hc in range(16):
            s2 = S2[hc % 2]
            S.I("dve", "max", [Ssb], [Tv], out=Tv[:, hc, 0:8], in_=Ssb[:, hc, :])
            S.I("dve", "max_index", [Ssb, Tv], [Ti], out=Ti[:, hc, 0:8], in_max=Tv[:, hc, 0:8], in_values=Ssb[:, hc, :])
            S.I("dve", "match_replace", [Ssb, Tv], [s2], out=s2[:], in_to_replace=Tv[:, hc, 0:8], in_values=Ssb[:, hc, :], imm_value=-1e30)
            S.I("dve", "max", [s2], [Tv], out=Tv[:, hc, 8:16], in_=s2[:])
            S.I("dve", "max_index", [s2, Tv], [Ti], out=Ti[:, hc, 8:16], in_max=Tv[:, hc, 8:16], in_values=s2[:])
        if CUT < 3:
            continue
        S.I("dve", "tensor_copy", [Ti], [Tif], out=Tif[:], in_=Ti[:])
        Tv4 = Tv[:].rearrange("p (h c) k -> p h c k", c=2)
        Tif4 = Tif[:].rearrange("p (h c) k -> p h c k", c=2)
        S.I("dve", "tensor_tensor", [Tv], [cand], out=cand[:], in0=Tv4[:, :, 0, :].unsqueeze(3).to_broadcast(B4), in1=Tv4[:, :, 1, :].unsqueeze(2).to_broadcast(B4), op=ALU.add)
        for h in range(8):
            cf = cand[:, h, :, :].rearrange("p a b -> p (a b)")
            c2 = cand2[h % 2]
            S.I("dve", "max", [cand], [Bv], out=Bv[:, h, 0:8], in_=cf)
            S.I("dve", "max_index", [cand, Bv], [Bp], out=Bp[:, h, 0:8], in_max=Bv[:, h, 0:8], in_values=cf)
            S.I("dve", "match_replace", [cand, Bv], [c2], out=c2[:], in_to_replace=Bv[:, h, 0:8], in_values=cf, imm_value=-1e30)
            S.I("dve", "max", [c2], [Bv], out=Bv[:, h, 8:16], in_=c2[:])
            S.I("dve", "max_index", [c2, Bv], [Bp], out=Bp[:, h, 8:16], in_max=Bv[:, h, 8:16], in_values=c2[:])
        if CUT < 4:
            continue
        S.I("dve", "tensor_copy", [Bp], [Af], out=Af[:], in_=Bp[:])
        posb = Af[:].unsqueeze(3).to_broadcast(B4)
        S.I("dve", "tensor_tensor", [Af, lo16], [eq], out=eq[:], in0=posb, in1=lo16.t, op=ALU.is_ge)
        S.I("dve", "tensor_tensor", [Af, hi16], [eq2], out=eq2[:], in0=posb, in1=hi16.t, op=ALU.is_ge)
        S.I("dve", "tensor_tensor", [eq, eq2], [eq], out=eq[:], in0=eq[:], in1=eq2[:], op=ALU.subtract)
        S.I("dve", "tensor_tensor", [eq, io4], [eq2], out=eq2[:], in0=eq[:], in1=io4.t, op=ALU.mult)
        S.I("dve", "tensor_reduce", [eq2], [Akf], out=Akf[:], in_=eq2[:], axis=AX.X, op=ALU.add)
        S.I("dve", "tensor_tensor", [eq, Tif], [eq], out=eq[:], in0=eq[:], in1=Tif4[:, :, 0, :].unsqueeze(2).to_broadcast(B4), op=ALU.mult)
        S.I("dve", "tensor_reduce", [eq], [Ik], out=Ik[:], in_=eq[:], axis=AX.X, op=ALU.add)
        S.I("dve", "scalar_tensor_tensor", [Akf, Af], [Bf], out=Bf[:].rearrange("p h k -> p (h k)"), in0=Akf[:].rearrange("p h k -> p (h k)"), scalar=-16.0, in1=Af[:].rearrange("p h k -> p (h k)"), op0=ALU.mult, op1=ALU.add)
        S.I("dve", "tensor_tensor", [io4, Bf], [eq], out=eq[:], in0=io4.t, in1=Bf[:].unsqueeze(3).to_broadcast(B4), op=ALU.is_equal)
        S.I("dve", "tensor_tensor", [eq, Tif], [eq], out=eq[:], in0=eq[:], in1=Tif4[:, :, 1, :].unsqueeze(2).to_broadcast(B4), op=ALU.mult)
        S.I("dve", "tensor_reduce", [eq], [Jk], out=Jk[:], in_=eq[:], axis=AX.X, op=ALU.add)
        S.I("dve", "scalar_tensor_tensor", [Ik, Jk], [Ef], out=Ef[:], in0=Ik[:].rearrange("p h k -> p (h k)"), scalar=128.0, in1=Jk[:].rearrange("p h k -> p (h k)"), op0=ALU.mult, op1=ALU.add)
        S.I("dve", "tensor_copy", [Ef], [Ei], out=Ei[:], in_=Ef[:])
        if CUT < 5:
            continue
        S.I("dve", "tensor_tensor", [Bv], [Gt], out=Gt[:], in0=Bv[:], in1=Bv[:, :, 0:1].to_broadcast([128, 8, 16]), op=ALU.subtract)
        S.I("act", "activation", [Gt], [Gt], out=Gt[:], in_=Gt[:], func=AF.Exp)
        S.I("dve", "tensor_reduce", [Gt], [sm], out=sm[:], in_=Gt[:], axis=AX.X, op=ALU.add)
        S.I("dve", "reciprocal", [sm], [sm], out=sm[:], in_=sm[:])
        S.I("dve", "tensor_tensor", [Gt, sm], [Gt], out=Gt[:], in0=Gt[:], in1=sm[:].unsqueeze(2).to_broadcast([128, 8, 16]), op=ALU.mult)
        if CUT < 6:
            continue
        S.I("dve", "memset", [], [av], av[:], 0.0)
        for c in range(128):
            rw = rows[nrow % NROW]
            nrow += 1
            S.dma("pool", (lambda e, rw=rw, c=c: e.indirect_dma_start(out=rw[:], out_offset=None, in_=d.ed_bf[:, :], in_offset=bass.IndirectOffsetOnAxis(ap=Ei[:, c:c + 1], axis=0))), rw, srcs=[Ei, d.ed_bf])
            S.I("dve", "scalar_tensor_tensor", [rw, hf3], [junk, av], out=junk[:], in0=rw[:], scalar=1.0, in1=hf3[:, 0, :], op0=ALU.mult, op1=ALU.mult, accum_out=av[:, c:c + 1])
        if CUT < 7:
            continue
        S.I("dve", "tensor_tensor", [av], [g1], out=g1[:], in0=av[:], in1=av[:], op=ALU.mult)
        S.I("dve", "tensor_scalar", [g1], [g1], out=g1[:], in0=g1[:], scalar1=0.044715, scalar2=1.0, op0=ALU.mult, op1=ALU.add)
        S.I("dve", "tensor_tensor", [g1, av], [g1], out=g1[:], in0=g1[:], in1=av[:], op=ALU.mult)
        S.I("act", "activation", [g1], [g2], out=g2[:], in_=g1[:], func=AF.Sigmoid, scale=1.5957691216057308)
        S.I("dve", "tensor_tensor", [g2, av], [g2], out=g2[:], in0=g2[:], in1=av[:], op=ALU.mult)
        S.I("dve", "tensor_tensor", [g2, Gt], [Wt], out=Wt[:], in0=g2[:], in1=Gt[:].rearrange("p h k -> p (h k)"), op=ALU.mult)
        for c in range(128):
            rw = rows[nrow % NROW]
            nrow += 1
            S.dma("pool", (lambda e, rw=rw, c=c: e.indirect_dma_start(out=rw[:], out_offset=None, in_=d.eu_bf[:, :], in_offset=bass.IndirectOffsetOnAxis(ap=Ei[:, c:c + 1], axis=0))), rw, srcs=[Ei, d.eu_bf])
            S.I("dve", "scalar_tensor_tensor", [rw, Wt, x2], [x2], out=x2[:, 0, :], in0=rw[:], scalar=Wt[:, c:c + 1], in1=x2[:, 0, :], op0=ALU.mult, op1=ALU.add)
        if d.last:
            rms_tile(S, x2, x2, 1, gn, [xnb], ssq, rs, junk)
            S.D("sp", d.xn, [xnb], d.xn[r0:r0 + 128, :], xnb[:, 0, :])
        else:
            S.D("sp", d.xo, [x2], d.xo[r0:r0 + 128, :], x2[:, 0, :])
    if d.last:
        S.fence("sp", [d.xn])
    S.pop()


def phase3(S, d, NSEQ, ident, ntiles=None):
    import os
    sub = os.environ.get("P3", "abc")
    if "a" in sub:
        phase3a(S, d, NSEQ, ident)
    if "b" in sub:
        phase3b(S, d, NSEQ, ident, ntiles)
    if "c" in sub:
        phase3c(S, d, NSEQ, ident, ntiles)


def build_prog(NSEQ, depth=4, phases=(1, 2, 3), ntiles=None, dbg=False, heads=range(16), tts=range(4)):
    nc = bass.Bass("TRN2", target_bir_lowering=False)
    S = Sched(nc)
    d = declare(S, NSEQ, dbg, big=(3 in phases), depth=depth)
    ident = make_ident(S)
    for L in range(depth):
        v = layer_view(d, L)
        if 1 in phases:
            phase1(S, v, NSEQ, ident)
        if 2 in phases:
            phase2(S, v, NSEQ, heads, tts)
        if 3 in phases:
            phase0(S, v)
            phase3(S, v, NSEQ, ident, ntiles)
    S.emit()
    print("[build] ops:", {e: len(S.ops[e]) for e in ENGS}, "dma sems:", S.nsem, flush=True)
    S.close()
    return nc


N_CORES = 8
DEPTH = 4
_NC_CACHE = {}


def _stacked_weights(inputs):
    g = lambda k: np.ascontiguousarray(np.asarray(inputs[k]), dtype=np.float32)
    D = g("norm_mix").shape[0]
    return {
        "norm_mix": g("norm_mix").reshape(D, 1, DM),
        "w_in": g("w_in"),
        "decay": g("ret_decay_logit").reshape(D, 1, 16),
        "ret_gain": np.ascontiguousarray(g("ret_norm_gain").reshape(D, 8, 64).transpose(0, 2, 1)),
        "att_gain": np.ascontiguousarray(g("att_norm_gain").reshape(D, 8, 64).transpose(0, 2, 1)),
        "w_out": g("w_out"),
        "norm_mem": g("norm_mem").reshape(D, 1, DM),
        "norm_mem_kv": g("norm_mem_kv").reshape(D, 1, DM),
        "w_mem_q": g("w_mem_q"),
        "w_mem_kv": g("w_mem_kv"),
        "w_mem_o": g("w_mem_o"),
        "norm_ffn": g("norm_ffn").reshape(D, 1, DM),
        "peer_wq": g("peer_w_query"),
        "sub_keys": g("peer_sub_keys").reshape(D, 256, 128),
        "e_down": g("peer_expert_down"),
        "e_up": g("peer_expert_up"),
        "norm_final": g("norm_final").reshape(1, DM),
    }


def kernel(**inputs):
    x = np.asarray(inputs["x"], dtype=np.float32)
    mem = np.asarray(inputs["mem"], dtype=np.float32)
    B = x.shape[0]
    nseq = B // N_CORES
    if nseq not in _NC_CACHE:
        _NC_CACHE[nseq] = build_prog(nseq, DEPTH)
    nc = _NC_CACHE[nseq]
    w = _stacked_weights(inputs)
    in_maps = []
    for c in range(N_CORES):
        m = dict(w)
        m["x"] = np.ascontiguousarray(x[c * nseq:(c + 1) * nseq].reshape(nseq * SEQ, DM))
        m["mem"] = np.ascontiguousarray(mem[c * nseq:(c + 1) * nseq].reshape(nseq * MEM, DM))
        in_maps.append(m)
    res = run_bass_kernel_spmd(nc, in_maps, core_ids=list(range(N_CORES)))
    out = np.concatenate([np.asarray(res.results[c]["xn"]).reshape(nseq, SEQ, DM) for c in range(N_CORES)], axis=0)
    return out.astype(np.float32)
```

```python
import numpy as np
from contextlib import ExitStack

import concourse.bass as bass
import concourse.mybir as mybir
from concourse.bass_utils import run_bass_kernel_spmd

F32 = mybir.dt.float32
BF16 = mybir.dt.bfloat16
I32 = mybir.dt.int32
U32 = mybir.dt.uint32
ALU = mybir.AluOpType
AF = mybir.ActivationFunctionType
AX = mybir.AxisListType


class Res:
    __slots__ = ("name", "last_w", "readers", "dma_sem", "dma_cnt", "excl", "old_sems")

    def __init__(self, name):
        self.name = name
        self.excl = False
        self.old_sems = []
        self.last_w = None
        self.readers = []
        self.dma_sem = None
        self.dma_cnt = 0

    def dma_snapshot(self):
        return tuple(self.old_sems) + ((self.dma_sem, self.dma_cnt),)


class Op:
    __slots__ = ("eng", "fn", "deps", "flag", "cnt", "dma_res", "dma_val", "dma_sem", "sem")

    def __init__(self, eng, fn):
        self.eng = eng
        self.fn = fn
        self.deps = []
        self.flag = False
        self.cnt = 0
        self.dma_res = None
        self.dma_val = 0


ENGS = ("pe", "act", "dve", "pool", "sp")
import os as _os
SEM_LIMIT = int(_os.environ.get("SEM_LIMIT", "24000"))


class Sched:
    def __init__(self, nc):
        self.nc = nc
        self.stack = ExitStack()
        self.root = self.stack
        self.ops = {e: [] for e in ENGS}
        self.nres = 0
        self.nsem = 0
        self.free_sems = []
        self.all_res = []
        self.bar_tok = None
        self.scopes = []

    def res(self, name=None):
        self.nres += 1
        r = Res("%s_%d" % (name or "r", self.nres))
        r.last_w = self.bar_tok
        self.all_res.append(r)
        return r

    def push(self):
        self.scopes.append((self.stack, len(self.all_res)))
        self.stack = ExitStack()

    def pop(self):
        self.barrier()
        self.stack.close()
        self.stack, n0 = self.scopes.pop()
        for r in self.all_res[n0:]:
            if r.dma_sem is not None and r.name.startswith("sb_"):
                self.free_sems.append((r.dma_sem, r.dma_cnt))
                r.dma_sem = None
        self.all_res = self.all_res[:n0]

    def barrier(self):
        for e in ENGS:
            o = self.op(e, (lambda eng: eng.nop()), reads=(), writes=list(self.all_res))
        self.bar_tok = ("op", o, None)

    def I(self, eng, name, reads, writes, *args, **kw):
        return self.op(eng, (lambda e: getattr(e, name)(*args, **kw)), reads=reads, writes=writes)

    def D(self, eng, dst, srcs, out, in_, **kw):
        return self.dma(eng, (lambda e: e.dma_start(out=out, in_=in_, **kw)), dst, srcs=srcs)

    def sb(self, name, shape, dtype):
        self.nres += 1
        name = "sb_%s_%d" % (name, self.nres)
        t = self.stack.enter_context(self.nc.sbuf_tensor(name, list(shape), dtype))
        t_res = self.res(name)
        return T(t, t_res)

    def ps(self, name, shape, dtype=F32):
        self.nres += 1
        name = "ps_%s_%d" % (name, self.nres)
        t = self.stack.enter_context(self.nc.psum_tensor(name, list(shape), dtype))
        r = self.res(name)
        r.excl = True
        return T(t, r)

    def dram(self, name, shape, dtype, kind="Internal"):
        t = self.nc.dram_tensor(name, list(shape), dtype, kind=kind)
        return T(t.ap(), self.res(name))

    def _dep_tokens(self, op, reads, writes):
        toks = []
        for r in reads:
            if r.last_w is not None:
                toks.append(r.last_w)
            if r.excl:
                toks.extend(tk for tk in r.readers if tk[0] == "op" and tk[1].eng != op.eng)
        for w in writes:
            if w.last_w is not None:
                toks.append(w.last_w)
            toks.extend(w.readers)
        for tk in toks:
            if tk[0] == "op":
                if tk[1].eng == op.eng and op.eng in ("pe", "sp"):
                    continue
                op.deps.append(tk)
            else:
                op.deps.append(("dma", tk[1], tk[1].dma_snapshot()))

    def op(self, eng, fn, reads=(), writes=()):
        reads = [r.res if isinstance(r, T) else r for r in reads]
        writes = [w.res if isinstance(w, T) else w for w in writes]
        o = Op(eng, fn)
        self._dep_tokens(o, reads, writes)
        tok = ("op", o, None)
        if fn is not None:
            for r in reads:
                r.readers.append(tok)
            for w in writes:
                w.last_w = tok
                w.readers = []
        self.ops[eng].append(o)
        return o

    def dma(self, eng, fn, dst, srcs=(), extra_reads=()):
        dres = dst.res if isinstance(dst, T) else dst
        reads = [r.res if isinstance(r, T) else r for r in list(srcs) + list(extra_reads)]
        o = Op(eng, fn)
        self._dep_tokens(o, reads, [dres])
        if dres.dma_sem is not None and dres.dma_cnt + 16 > SEM_LIMIT:
            dres.old_sems.append((dres.dma_sem, dres.dma_cnt))
            dres.dma_sem = None
        if dres.dma_sem is None:
            while self.free_sems and self.free_sems[-1][1] + 16 > SEM_LIMIT:
                self.free_sems.pop()
            if self.free_sems:
                dres.dma_sem, dres.dma_cnt = self.free_sems.pop()
            else:
                self.nsem += 1
                dres.dma_sem = self.root.enter_context(self.nc.semaphore("d%d_%s" % (self.nsem, dres.name)))
                dres.dma_cnt = 0
        dres.dma_cnt += 16
        o.dma_res = dres
        o.dma_val = dres.dma_cnt
        o.dma_sem = dres.dma_sem
        tok = ("dma", dres, dres.dma_cnt)
        for r in reads:
            r.readers.append(tok)
        dres.last_w = tok
        dres.readers = []
        self.ops[eng].append(o)
        return o

    def fence(self, eng, reads):
        return self.op(eng, None, reads=reads, writes=())

    def emit(self):
        nc = self.nc
        for e in ENGS:
            for o in self.ops[e]:
                for d in o.deps:
                    if d[0] == "op":
                        d[1].flag = True
        for e in ENGS:
            c = 0
            ep = 0
            sem = None
            for o in self.ops[e]:
                if o.flag:
                    if sem is None or c >= SEM_LIMIT:
                        ep += 1
                        sem = self.root.enter_context(nc.semaphore("s_%s_%d" % (e, ep)))
                        c = 0
                    c += 1
                    o.cnt = c
                    o.sem = sem
        ops = self.ops

        def run(e, engine):
            waited = {}
            for o in ops[e]:
                need = {}
                for d in o.deps:
                    if d[0] == "op":
                        pairs = ((d[1].sem, d[1].cnt),)
                    else:
                        pairs = d[2]
                    for (s_, v) in pairs:
                        k = id(s_)
                        if waited.get(k, 0) >= v:
                            continue
                        if k not in need or need[k][1] < v:
                            need[k] = (s_, v)
                for k, (s_, v) in need.items():
                    engine.wait_ge(s_, v)
                    waited[k] = v
                if o.fn is None:
                    continue
                ins = o.fn(engine)
                if o.dma_res is not None:
                    ins.then_inc(o.dma_sem, 16)
                elif o.flag:
                    ins.then_inc(o.sem, 1)

        with nc.Block() as block:
            @block.tensor
            def _(eng):
                run("pe", eng)

            @block.scalar
            def _(eng):
                run("act", eng)

            @block.vector
            def _(eng):
                run("dve", eng)

            @block.gpsimd
            def _(eng):
                run("pool", eng)

            @block.sync
            def _(eng):
                run("sp", eng)

    def close(self):
        self.stack.close()


class T:
    __slots__ = ("t", "res")

    def __init__(self, t, res):
        self.t = t
        self.res = res

    def __getitem__(self, k):
        return self.t[k]


import math

SEQ = 2048
DM = 1024
MEM = 256
TW = 3968
TOFF = 1920


class Dd:
    pass


def declare(S, NSEQ, dbg=False, big=True, depth=1):
    d = Dd()
    NT = NSEQ * SEQ
    ein = lambda n, shp, dt=F32: S.dram(n, [depth] + shp, dt, kind="ExternalInput")
    d.depth = depth
    d.x_in = S.dram("x", [NT, DM], F32, kind="ExternalInput")
    d.mem = S.dram("mem", [NSEQ * MEM, DM], F32, kind="ExternalInput")
    d.st = {}
    d.st["norm_mix"] = ein("norm_mix", [1, DM])
    d.st["w_in"] = ein("w_in", [DM, 3584])
    d.st["decay"] = ein("decay", [1, 16])
    d.st["ret_gain"] = ein("ret_gain", [64, 8])
    d.st["att_gain"] = ein("att_gain", [64, 8])
    d.st["w_out"] = ein("w_out", [DM, DM])
    d.st["norm_mem"] = ein("norm_mem", [1, DM])
    d.st["norm_mem_kv"] = ein("norm_mem_kv", [1, DM])
    d.st["w_mem_q"] = ein("w_mem_q", [DM, DM])
    d.st["w_mem_kv"] = ein("w_mem_kv", [DM, 2 * DM])
    d.st["w_mem_o"] = ein("w_mem_o", [DM, DM])
    d.st["norm_ffn"] = ein("norm_ffn", [1, DM])
    d.st["peer_wq"] = ein("peer_wq", [DM, 2048])
    d.st["sub_keys"] = ein("sub_keys", [256, 128])
    if big:
        d.st["e_down"] = ein("e_down", [16384, DM])
        d.st["e_up"] = ein("e_up", [16384, DM])
    d.norm_final = S.dram("norm_final", [1, DM], F32, kind="ExternalInput")
    d.xn = S.dram("xn", [NT, DM], F32, kind="ExternalOutput")
    kd = "ExternalOutput" if dbg else "Internal"
    d.xbuf = [S.dram("xbuf%d" % i, [NT, DM], F32, kind=("ExternalOutput" if (dbg and i == 0) else "Internal")) for i in range(2)]
    d.projT = [S.dram("projT%d" % s, [2560, SEQ], BF16, kind=kd) for s in range(NSEQ)]
    d.vd = S.dram("vd", [NT, DM], BF16, kind=kd)
    d.cT = S.dram("cT", [DM, NT], BF16, kind=kd)
    d.x2 = S.dram("x2", [NT, DM], F32, kind=kd)
    if big:
        d.euv = S.dram("euv", [16384, 2 * DM], BF16)
    d.kt_d = S.dram("kt_d", [NSEQ, 128, 8, MEM], BF16)
    d.vm_d = S.dram("vm_d", [NSEQ, 128, 2, DM], BF16)
    return d


def layer_view(d, L):
    v = Dd()
    v.__dict__.update(d.__dict__)
    for k, t in d.st.items():
        setattr(v, k, T(t.t[L], t.res))
    last = (L == d.depth - 1)
    if "e_down" in d.st:
        v.e_down_all = T(d.st["e_down"].t.rearrange("l e c -> (l e) c"), d.st["e_down"].res)
        v.e_up_all = T(d.st["e_up"].t.rearrange("l e c -> (l e) c"), d.st["e_up"].res)
    v.ebase = float(L * 16384)
    v.x = d.x_in if L == 0 else d.xbuf[(L - 1) % 2]
    v.xo = d.xbuf[L % 2]
    v.last = last
    return v


def make_ident(S):
    ident = S.sb("ident", [128, 128], BF16)
    io_i = S.sb("idio_i", [128, 128], I32)
    io_f = S.sb("idio_f", [128, 128], F32)
    S.I("pool", "iota", [], [io_i], io_i[:], pattern=[[1, 128]], base=0, channel_multiplier=-1)
    S.I("dve", "tensor_copy", [io_i], [io_f], out=io_f[:], in_=io_i[:])
    S.I("dve", "tensor_scalar", [io_f], [ident], out=ident[:], in0=io_f[:], scalar1=0.0, scalar2=None, op0=ALU.is_equal)
    return ident


def load_w_bf16(S, name, src, kc, ncol, p=128):
    w = S.sb(name, [p, kc, ncol], BF16)
    v = src.t.rearrange("(k p) n -> p k n", p=p)
    for c0 in range(0, ncol, 512):
        c1 = min(ncol, c0 + 512)
        S.D("pool", w, [src], w[:, :, c0:c1], v[:, :, c0:c1])
    return w


def rms_tile(S, xin, xres, nsub, gb, hout, ssq, rs, junk):
    S.I("dve", "memset", [], [ssq], ssq[:], 0.0)
    for j in range(nsub):
        S.I("act", "activation", [xres], [junk, ssq], out=junk[:], in_=xin[:, j, :], func=AF.Square, accum_out=ssq[:, j:j + 1])
    S.I("act", "activation", [ssq], [rs], out=rs[:, 0:nsub], in_=ssq[:, 0:nsub], func=AF.Sqrt, bias=1e-6, scale=1.0 / DM)
    S.I("dve", "reciprocal", [rs], [rs], out=rs[:, 0:nsub], in_=rs[:, 0:nsub])
    for j in range(nsub):
        for ho in hout:
            S.I("dve", "scalar_tensor_tensor", [xres, rs, gb], [ho], out=ho[:, j, :], in0=xin[:, j, :], scalar=rs[:, j:j + 1], in1=gb[:], op0=ALU.mult, op1=ALU.mult)


def phase1(S, d, NSEQ, ident):
    S.push()
    w = load_w_bf16(S, "w_in", d.w_in, 8, 3584)
    gb = S.sb("gb1", [128, DM], F32)
    S.D("sp", gb, [d.norm_mix], gb[:], d.norm_mix[0:1, :].partition_broadcast(128))
    xs = [S.sb("xs%d" % i, [128, 4, DM], F32) for i in range(2)]
    junk = S.sb("junk1", [128, DM], BF16)
    ssq = [S.sb("ssq%d" % i, [128, 4], F32) for i in range(2)]
    rs = [S.sb("rs%d" % i, [128, 4], F32) for i in range(2)]
    h = S.sb("h1", [128, 4, DM], BF16)
    hT = [S.sb("hT%d" % i, [128, 8, 512], BF16) for i in range(2)]
    fo = [S.sb("fo%d" % i, [128, 512], BF16) for i in range(4)]
    pt = [S.ps("pt%d" % i, [128, 8, 128], BF16) for i in range(2)]
    pf = [S.ps("pf%d" % i, [128, 512], F32) for i in range(4)]
    fm = []
    for c in range(4):
        fm.append((c * 128, c * 128, False))
    for c in range(4):
        fm.append((512 + c * 128, 512 + c * 128, False))
    for c in range(4):
        fm.append((1536 + c * 128, 1024 + c * 128, True))
    for c in range(4):
        fm.append((2048 + c * 128, 1536 + c * 128, False))
    for c in range(4):
        fm.append((2560 + c * 128, 2048 + c * 128, False))
    nfo = 0
    npf = 0
    for s in range(NSEQ):
        for tt in range(4):
            i = s * 4 + tt
            x_ = xs[i % 2]
            hT_ = hT[i % 2]
            r0 = s * SEQ + tt * 512
            S.D("sp", x_, [d.x], x_[:], d.x.t[r0:r0 + 512, :].rearrange("(j p) c -> p j c", p=128))
            rms_tile(S, x_, x_, 4, gb, [h], ssq[i % 2], rs[i % 2], junk)
            for j in range(4):
                p_ = pt[j % 2]
                for k in range(8):
                    S.I("pe", "transpose", [h, ident], [p_], out=p_[:, k, :], in_=h[:, j, k * 128:(k + 1) * 128], identity=ident[:])
                if j % 2 == 0:
                    S.I("act", "copy", [p_], [hT_], out=hT_[:, :, j * 128:(j + 1) * 128], in_=p_[:])
                else:
                    S.I("dve", "tensor_copy", [p_], [hT_], out=hT_[:, :, j * 128:(j + 1) * 128], in_=p_[:])
            for (col, row, silu) in fm:
                p_ = pf[npf % 4]
                npf += 1
                f_ = fo[nfo % 4]
                nfo += 1
                for k in range(8):
                    S.I("pe", "matmul", [w, hT_], [p_], p_[:], lhsT=w[:, k, col:col + 128], rhs=hT_[:, k, :], start=(k == 0), stop=(k == 7))
                if silu:
                    S.I("act", "activation", [p_], [f_], out=f_[:], in_=p_[:], func=AF.Silu)
                elif nfo % 2 == 0:
                    S.I("act", "copy", [p_], [f_], out=f_[:], in_=p_[:])
                else:
                    S.I("dve", "tensor_copy", [p_], [f_], out=f_[:], in_=p_[:])
                S.D("pool", d.projT[s], [f_], d.projT[s][row:row + 128, tt * 512:(tt + 1) * 512], f_[:])
            for j in range(4):
                for vi, col in enumerate((1024, 3072)):
                    p_ = pf[npf % 4]
                    npf += 1
                    f_ = fo[nfo % 4]
                    nfo += 1
                    for k in range(8):
                        S.I("pe", "matmul", [w, hT_], [p_], p_[:], lhsT=hT_[:, k, j * 128:(j + 1) * 128], rhs=w[:, k, col:col + 512], start=(k == 0), stop=(k == 7))
                    if nfo % 2 == 0:
                        S.I("act", "copy", [p_], [f_], out=f_[:], in_=p_[:])
                    else:
                        S.I("dve", "tensor_copy", [p_], [f_], out=f_[:], in_=p_[:])
                    S.D("pool", d.vd, [f_], d.vd[r0 + j * 128:r0 + (j + 1) * 128, vi * 512:(vi + 1) * 512], f_[:])
    S.pop()


def phase2(S, d, NSEQ, heads=range(16), tts=range(4)):
    S.push()
    dl_i = S.sb("dl_i", [128, TW], I32)
    A = S.sb("tA", [128, TW], F32)
    B = S.sb("tB", [128, TW], F32)
    rpos = S.sb("rpos", [128, TW], F32)
    rneg = S.sb("rneg", [128, TW], F32)
    multb = S.sb("multb", [128, TW], BF16)
    tab = [S.sb("tab%d" % i, [128, TW], BF16) for i in range(2)]
    S.I("pool", "iota", [], [dl_i], dl_i[:], pattern=[[1, TW]], base=-TOFF, channel_multiplier=-1)
    S.I("dve", "tensor_copy", [dl_i], [A], out=A[:], in_=dl_i[:])
    S.I("dve", "tensor_scalar", [A], [rpos], out=rpos[:], in0=A[:], scalar1=0.0, scalar2=None, op0=ALU.max)
    S.I("dve", "tensor_scalar", [A], [rneg], out=rneg[:], in0=A[:], scalar1=-1.0, scalar2=0.0, op0=ALU.mult, op1=ALU.max)
    S.I("dve", "tensor_tensor", [rpos, rneg], [A], out=A[:], in0=rpos[:], in1=rneg[:], op=ALU.add)
    S.I("dve", "tensor_scalar", [A], [multb], out=multb[:], in0=A[:], scalar1=64.0, scalar2=None, op0=ALU.is_le)
    tmp_i = dl_i
    for (inv, lim) in ((0.25, 256.0), (0.0625, 1024.0)):
        S.I("dve", "tensor_scalar", [A], [B], out=B[:], in0=A[:], scalar1=inv, scalar2=None, op0=ALU.mult)
        S.I("dve", "tensor_copy", [B], [tmp_i], out=tmp_i[:], in_=B[:])
        S.I("dve", "tensor_tensor", [B, tmp_i], [B], out=B[:], in0=B[:], in1=tmp_i[:], op=ALU.is_equal)
        S.I("dve", "scalar_tensor_tensor", [A, B], [B], out=B[:], in0=A[:], scalar=lim, in1=B[:], op0=ALU.is_le, op1=ALU.mult)
        S.I("dve", "tensor_tensor", [multb, B], [multb], out=multb[:], in0=multb[:], in1=B[:], op=ALU.add)
    lg = S.sb("lg", [128, 16], F32)
    S.D("sp", lg, [d.decay], lg[:], d.decay[0:1, :].partition_broadcast(128))
    S.I("act", "activation", [lg], [lg], out=lg[:], in_=lg[:], func=AF.Exp, scale=-1.0)
    S.I("act", "activation", [lg], [lg], out=lg[:], in_=lg[:], func=AF.Ln, bias=1.0)
    S.I("dve", "tensor_scalar", [lg], [lg], out=lg[:], in0=lg[:], scalar1=-1.0, scalar2=None, op0=ALU.mult)
    rgain = S.sb("rgain", [64, 8], F32)
    again = S.sb("again", [64, 8], F32)
    S.D("sp", rgain, [d.ret_gain], rgain[:], d.ret_gain[:, :])
    S.D("sp", again, [d.att_gain], again[:], d.att_gain[:, :])
    ones64 = S.sb("ones64", [64, 64], BF16)
    S.I("dve", "memset", [], [ones64], ones64[:], 1.0)
    sel = S.sb("sel", [128, 64], BF16)
    S.I("dve", "memset", [], [sel], sel[:], 0.0)
    S.I("dve", "memset", [sel], [sel], sel[64:65, :], 1.0)
    qT = [S.sb("qT%d" % i, [64, SEQ], BF16) for i in range(2)]
    kT = [S.sb("kT%d" % i, [64, SEQ], BF16) for i in range(2)]
    gT = [S.sb("gT%d" % i, [64, SEQ], BF16) for i in range(2)]
    V = [S.sb("V%d" % i, [128, 16, 128], BF16) for i in range(2)]
    for i in range(2):
        S.I("dve", "memset", [], [V[i]], V[i][:], 0.0)
        S.I("dve", "memset", [V[i]], [V[i]], V[i][:, :, 64:65], 1.0)
    PT = [S.sb("PT%d" % i, [128, 512], BF16) for i in range(3)]
    E = [S.sb("E%d" % i, [128, 512], BF16) for i in range(2)]
    Osb = S.sb("Osb", [64, 512], F32)
    Ob = S.sb("Ob", [128, 512], BF16)
    S.I("dve", "memset", [], [Ob], Ob[:], 0.0)
    sq = S.sb("sq", [64, 512], BF16)
    t1 = S.sb("t1", [64, 512], F32)
    rstd = S.sb("rstd", [64, 512], F32)
    yb = S.sb("yb", [64, 512], F32)
    cTs = [S.sb("cTs%d" % i, [64, 512], BF16) for i in range(2)]
    psc = [S.ps("psc%d" % i, [128, 512], F32) for i in range(3)]
    po = [S.ps("po%d" % i, [128, 512], F32) for i in range(2)]
    pss = S.ps("pss", [64, 512], F32)
    pl = S.ps("pl", [64, 512], F32)
    LN8 = math.log(0.125)
    nb = 0
    nt = 0
    for s in range(NSEQ):
        for hh in heads:
            sl = (s * 16 + hh) % 2
            isret = hh < 8
            h = hh % 8
            if isret:
                qr, kr, gr, vc = h * 64, 512 + h * 64, 1024 + h * 64, h * 64
            else:
                qr, kr, gr, vc = 1536 + h * 64, 2048 + h * 64, None, 512 + h * 64
            pj = d.projT[s]
            S.D("sp", qT[sl], [pj], qT[sl][:], pj[qr:qr + 64, :])
            S.D("sp", kT[sl], [pj], kT[sl][:], pj[kr:kr + 64, :])
            if isret:
                S.D("sp", gT[sl], [pj], gT[sl][:], pj[gr:gr + 64, :])
            for c4 in range(4):
                S.D("sp", V[sl], [d.vd], V[sl][:, c4 * 4:(c4 + 1) * 4, 0:64], d.vd[s * SEQ + c4 * 512:s * SEQ + (c4 + 1) * 512, vc:vc + 64].rearrange("(c p) e -> p c e", p=128))
            tb = tab[sl]
            if isret:
                S.I("dve", "tensor_scalar", [rpos, lg], [B], out=B[:], in0=rpos[:], scalar1=lg[:, h:h + 1], scalar2=None, op0=ALU.mult)
                S.I("dve", "scalar_tensor_tensor", [rneg, lg, B], [B], out=B[:], in0=rneg[:], scalar=lg[:, 8 + h:9 + h], in1=B[:], op0=ALU.mult, op1=ALU.add)
                S.I("act", "activation", [B], [tb], out=tb[:], in_=B[:], func=AF.Exp, bias=LN8)
            else:
                slope = 2.0 ** (-(h + 1))
                S.I("act", "activation", [A], [tb], out=tb[:], in_=A[:], func=AF.Exp, scale=-slope)
                S.I("dve", "tensor_tensor", [tb, multb], [tb], out=tb[:], in0=tb[:], in1=multb[:], op=ALU.mult)
            M = 64 if isret else 128
            for tt in tts:
                T0 = tt * 512
                blocks = []
                for c in range(16):
                    S0 = c * 128
                    if (not isret) and (T0 - S0 - 127 > 1024 or T0 - S0 + 511 < -1024):
                        continue
                    blocks.append(c)
                po_ = po[nt % 2]
                nt += 1
                for bi, c in enumerate(blocks):
                    S0 = c * 128
                    u0 = T0 - S0 + TOFF
                    sc = psc[nb % 3]
                    pt_ = PT[nb % 3]
                    e_ = E[nb % 2]
                    nb += 1
                    S.I("pe", "matmul", [kT[sl], qT[sl]], [sc], sc[:], lhsT=kT[sl][:, S0:S0 + 128], rhs=qT[sl][:, T0:T0 + 512], start=True, stop=True)
                    if isret:
                        S.I("dve", "tensor_tensor", [sc, tb], [pt_], out=pt_[:], in0=sc[:], in1=tb[:, u0:u0 + 512], op=ALU.mult)
                    else:
                        S.I("act", "activation", [sc], [e_], out=e_[:], in_=sc[:], func=AF.Exp, scale=0.125)
                        S.I("dve", "tensor_tensor", [e_, tb], [pt_], out=pt_[:], in0=e_[:], in1=tb[:, u0:u0 + 512], op=ALU.mult)
                    S.I("pe", "matmul", [V[sl], pt_], [po_], po_[0:M, :], lhsT=V[sl][:, c, 0:M], rhs=pt_[:], start=(bi == 0), stop=(bi == len(blocks) - 1))
                import os as _os
                if _os.environ.get("NOEPI"):
                    continue
                gcol = rgain[:, h:h + 1] if isret else again[:, h:h + 1]
                gres = rgain if isret else again
                ct_ = cTs[nt % 2]
                S.I("act", "copy", [po_], [Osb], out=Osb[:], in_=po_[0:64, :])
                S.I("act", "activation", [po_], [sq], out=sq[:], in_=po_[0:64, :], func=AF.Square)
                S.I("pe", "matmul", [ones64, sq], [pss], pss[:], lhsT=ones64[:], rhs=sq[:], start=True, stop=True)
                if isret:
                    S.I("act", "activation", [pss], [rstd], out=rstd[:], in_=pss[:], func=AF.Sqrt, bias=1e-6, scale=1.0 / 64)
                else:
                    lvl = int(_os.environ.get("EPI", "9"))
                    if lvl >= 0:
                        S.I("act", "copy", [po_], [Ob], out=Ob[:], in_=po_[:])
                    if lvl >= 1:
                        S.I("pe", "matmul", [sel, Ob], [pl], pl[:], lhsT=sel[:], rhs=Ob[:], start=True, stop=True)
                    if lvl >= 2:
                        S.I("act", "activation", [pl], [t1], out=t1[:], in_=pl[:], func=AF.Square, scale=1e-3)
                    if lvl >= 3:
                        S.I("dve", "scalar_tensor_tensor", [pss, t1], [t1], out=t1[:], in0=pss[:], scalar=1.0 / 64, in1=t1[:], op0=ALU.mult, op1=ALU.add)
                    if lvl >= 4:
                        S.I("act", "activation", [t1], [rstd], out=rstd[:], in_=t1[:], func=AF.Sqrt)
                    if lvl < 9:
                        continue
                S.I("dve", "reciprocal", [rstd], [rstd], out=rstd[:], in_=rstd[:])
                if isret:
                    S.I("dve", "scalar_tensor_tensor", [Osb, gres, rstd], [yb], out=yb[:], in0=Osb[0:64, :], scalar=gcol, in1=rstd[:], op0=ALU.mult, op1=ALU.mult)
                    S.I("dve", "tensor_tensor", [yb, gT[sl]], [ct_], out=ct_[:], in0=yb[:], in1=gT[sl][:, T0:T0 + 512], op=ALU.mult)
                else:
                    S.I("dve", "scalar_tensor_tensor", [Osb, gres, rstd], [ct_], out=ct_[:], in0=Osb[0:64, :], scalar=gcol, in1=rstd[:], op0=ALU.mult, op1=ALU.mult)
                S.D("pool", d.cT, [ct_], d.cT[hh * 64:(hh + 1) * 64, s * SEQ + T0:s * SEQ + T0 + 512], ct_[:])
    S.pop()


def phase0(S, d):
    S.push()
    buf = [S.sb("cv%d" % i, [128, 16, DM], BF16) for i in range(2)]
    n = 0
    dst = d.euv
    for src, c0 in ((d.e_down, 0), (d.e_up, DM)):
        for ch in range(8):
            b_ = buf[n % 2]
            n += 1
            sv = src.t[ch * 2048:(ch + 1) * 2048, :].rearrange("(p j) c -> p j c", p=128)
            dv = dst.t[ch * 2048:(ch + 1) * 2048, c0:c0 + DM].rearrange("(p j) c -> p j c", p=128)
            for q in range(4):
                S.D("pool", b_, [src], b_[:, q * 4:(q + 1) * 4, :], sv[:, q * 4:(q + 1) * 4, :])
            S.D("sp", dst, [b_], dv, b_[:])
    S.pop()


def phase3a(S, d, NSEQ, ident):
    S.push()
    w_kv = load_w_bf16(S, "w_mkv", d.w_mem_kv, 8, 2 * DM)
    gkv = S.sb("g_kv", [128, DM], F32)
    S.D("sp", gkv, [d.norm_mem_kv], gkv[:], d.norm_mem_kv[0:1, :].partition_broadcast(128))
    junk = S.sb("junk3a", [128, DM], F32)
    ssq = S.sb("ssq3a", [128, 4], F32)
    rs = S.sb("rs3a", [128, 4], F32)
    mx = S.sb("mx", [128, 2, DM], F32)
    mh = S.sb("mh", [128, 2, DM], BF16)
    mT = S.sb("mT", [128, 8, MEM], BF16)
    KT = S.sb("KTs", [128, 8, MEM], BF16)
    VM = S.sb("VMs", [128, 2, DM], BF16)
    p_t = S.ps("p_t3a", [128, 8, 128], BF16)
    p_a = S.ps("p_a3a", [128, MEM], F32)
    p_y = S.ps("p_y3a", [128, DM], F32)
    for s in range(NSEQ):
        S.D("sp", mx, [d.mem], mx[:], d.mem.t[s * MEM:(s + 1) * MEM, :].rearrange("(j p) c -> p j c", p=128))
        rms_tile(S, mx, mx, 2, gkv, [mh], ssq, rs, junk)
        for j in range(2):
            for k in range(8):
                S.I("pe", "transpose", [mh, ident], [p_t], out=p_t[:, k, :], in_=mh[:, j, k * 128:(k + 1) * 128], identity=ident[:])
            S.I("act", "copy", [p_t], [mT], out=mT[:, :, j * 128:(j + 1) * 128], in_=p_t[:])
        for f in range(8):
            for k in range(8):
                S.I("pe", "matmul", [w_kv, mT], [p_a], p_a[:], lhsT=w_kv[:, k, f * 128:(f + 1) * 128], rhs=mT[:, k, :], start=(k == 0), stop=(k == 7))
            S.I("act", "copy", [p_a], [KT], out=KT[:, f, :], in_=p_a[:])
        for j in range(2):
            for hf in range(2):
                for k in range(8):
                    S.I("pe", "matmul", [w_kv, mT], [p_y], p_y[:, hf * 512:(hf + 1) * 512], lhsT=mT[:, k, j * 128:(j + 1) * 128], rhs=w_kv[:, k, DM + hf * 512:DM + (hf + 1) * 512], start=(k == 0), stop=(k == 7))
            S.I("dve", "tensor_copy", [p_y], [VM], out=VM[:, j, :], in_=p_y[:])
        S.D("sp", d.kt_d, [KT], d.kt_d[s], KT[:])
        S.D("sp", d.vm_d, [VM], d.vm_d[s], VM[:])
    S.pop()


def phase3b(S, d, NSEQ, ident, ntiles=None):
    S.push()
    NT = NSEQ * SEQ
    w_out = load_w_bf16(S, "w_out", d.w_out, 16, DM, p=64)
    w_q = load_w_bf16(S, "w_mq", d.w_mem_q, 8, DM)
    w_o = load_w_bf16(S, "w_mo", d.w_mem_o, 8, DM)
    gm = S.sb("g_mem", [128, DM], F32)
    S.D("sp", gm, [d.norm_mem], gm[:], d.norm_mem[0:1, :].partition_broadcast(128))
    junk = S.sb("junk3b", [128, DM], F32)
    ssq = S.sb("ssq3b", [128, 4], F32)
    rs = S.sb("rs3b", [128, 4], F32)
    p_y = [S.ps("p_y%d" % i, [128, DM], F32) for i in range(2)]
    p_t = S.ps("p_t", [128, 8, 128], BF16)
    p_a = S.ps("p_a", [128, 8, 128], F32)
    p_l = S.ps("p_l", [128, 4, 128], F32)
    ones = S.sb("ones3", [128, 128], BF16)
    S.I("dve", "memset", [], [ones], ones[:], 1.0)
    KT = S.sb("KT", [128, 8, MEM], BF16)
    VM = S.sb("VM", [128, 2, DM], BF16)
    xt = [S.sb("xt%d" % i, [128, 1, DM], F32) for i in range(2)]
    cTt = [S.sb("cTt%d" % i, [64, 16, 128], BF16) for i in range(2)]
    hb = S.sb("hb", [128, 1, DM], BF16)
    hT = S.sb("hT3", [128, 8, 128], BF16)
    qT = S.sb("qT3", [128, 8, 128], BF16)
    PTm = S.sb("PTm", [128, 8, 128], BF16)
    rL = S.sb("rL", [128, 4, 128], F32)
    oT = S.sb("oT", [128, 8, 128], BF16)
    ntl = NT // 128 if ntiles is None else ntiles
    for ti in range(ntl):
        s = ti // 16
        r0 = ti * 128
        ct_ = cTt[ti % 2]
        x_ = xt[ti % 2]
        if ti % 16 == 0:
            S.D("sp", KT, [d.kt_d], KT[:], d.kt_d[s])
            S.D("sp", VM, [d.vm_d], VM[:], d.vm_d[s])
        S.D("sp", x_, [d.x], x_[:, 0, :], d.x[r0:r0 + 128, :])
        for q4 in range(4):
            S.D("sp", ct_, [d.cT], ct_[:, q4 * 4:(q4 + 1) * 4, :], d.cT.t[q4 * 256:(q4 + 1) * 256, r0:r0 + 128].rearrange("(h e) t -> e h t", e=64))
        py = p_y[0]
        for hf in range(2):
            for hh in range(16):
                S.I("pe", "matmul", [ct_, w_out], [py], py[:, hf * 512:(hf + 1) * 512], lhsT=ct_[:, hh, :], rhs=w_out[:, hh, hf * 512:(hf + 1) * 512], start=(hh == 0), stop=(hh == 15))
        S.I("dve", "tensor_tensor", [py, x_], [x_], out=x_[:, 0, :], in0=py[:], in1=x_[:, 0, :], op=ALU.add)
        rms_tile(S, x_, x_, 1, gm, [hb], ssq, rs, junk)
        for k in range(8):
            S.I("pe", "transpose", [hb, ident], [p_t], out=p_t[:, k, :], in_=hb[:, 0, k * 128:(k + 1) * 128], identity=ident[:])
        S.I("act", "copy", [p_t], [hT], out=hT[:], in_=p_t[:])
        for f in range(8):
            for k in range(8):
                S.I("pe", "matmul", [w_q, hT], [p_a], p_a[:, f, :], lhsT=w_q[:, k, f * 128:(f + 1) * 128], rhs=hT[:, k, :], start=(k == 0), stop=(k == 7))
        S.I("act", "copy", [p_a], [qT], out=qT[:], in_=p_a[:])
        pb = p_y[1]
        pbv = pb[:].rearrange("p (a b) -> p a b", b=128)
        for h in range(4):
            for mc in range(2):
                for dc in range(2):
                    S.I("pe", "matmul", [KT, qT], [pb], pbv[:, h * 2 + mc, :], lhsT=KT[:, 2 * h + dc, mc * 128:(mc + 1) * 128], rhs=qT[:, 2 * h + dc, :], start=(dc == 0), stop=(dc == 1))
        S.I("act", "activation", [pb], [PTm], out=PTm[:], in_=pbv, func=AF.Exp, scale=1.0 / 16)
        for h in range(4):
            for dc in range(2):
                for mc in range(2):
                    S.I("pe", "matmul", [VM, PTm], [p_a], p_a[:, 2 * h + dc, :], lhsT=VM[:, mc, (2 * h + dc) * 128:(2 * h + dc + 1) * 128], rhs=PTm[:, h * 2 + mc, :], start=(mc == 0), stop=(mc == 1))
            for mc in range(2):
                S.I("pe", "matmul", [ones, PTm], [p_l], p_l[:, h, :], lhsT=ones[:], rhs=PTm[:, h * 2 + mc, :], start=(mc == 0), stop=(mc == 1))
        S.I("dve", "reciprocal", [p_l], [rL], out=rL[:], in_=p_l[:])
        for h in range(4):
            S.I("dve", "tensor_tensor", [p_a, rL], [oT], out=oT[:, 2 * h:2 * h + 2, :], in0=p_a[:, 2 * h:2 * h + 2, :], in1=rL[:, h:h + 1, :].to_broadcast([128, 2, 128]), op=ALU.mult)
        for hf in range(2):
            for f in range(8):
                S.I("pe", "matmul", [oT, w_o], [py], py[:, hf * 512:(hf + 1) * 512], lhsT=oT[:, f, :], rhs=w_o[:, f, hf * 512:(hf + 1) * 512], start=(f == 0), stop=(f == 7))
        S.I("dve", "tensor_tensor", [py, x_], [x_], out=x_[:, 0, :], in0=py[:], in1=x_[:, 0, :], op=ALU.add)
        S.D("sp", d.x2, [x_], d.x2[r0:r0 + 128, :], x_[:, 0, :])
    S.pop()


def phase3c(S, d, NSEQ, ident, ntiles=None):
    S.push()
    NT = NSEQ * SEQ
    import os
    SK = os.environ.get("SK", "")
    if "w" not in SK:
        w_pq = load_w_bf16(S, "w_pq", d.peer_wq, 8, 2048)
    gf = S.sb("g_ffn", [128, DM], F32)
    gn = S.sb("g_fin", [128, DM], F32)
    if "g" not in SK:
        S.D("sp", gf, [d.norm_ffn], gf[:], d.norm_ffn[0:1, :].partition_broadcast(128))
        S.D("sp", gn, [d.norm_final], gn[:], d.norm_final[0:1, :].partition_broadcast(128))
    junk = S.sb("junk3c", [128, DM], F32)
    ssq = S.sb("ssq3c", [128, 4], F32)
    rs = S.sb("rs3c", [128, 4], F32)
    p_t = S.ps("p_tc", [128, 8, 128], BF16)
    p_a = S.ps("p_ac", [128, 8, 128], F32)
    p_b = S.ps("p_bc", [128, 8, 128], F32)
    keysT = S.sb("keysT", [128, 2, 128], BF16)
    kf = S.sb("kf", [128, 2, 128], F32)
    kb = S.sb("kb", [128, 2, 128], BF16)
    import os
    SK = os.environ.get("SK", "")
    if "k" not in SK:
        S.D("sp", kf, [d.sub_keys], kf[:], d.sub_keys.t.rearrange("(c k) e -> k c e", k=128))
        S.I("dve", "tensor_copy", [kf], [kb], out=kb[:], in_=kf[:])
        for c in range(2):
            S.I("pe", "transpose", [kb, ident], [p_t], out=p_t[:, c, :], in_=kb[:, c, :], identity=ident[:])
        S.I("act", "copy", [p_t], [keysT], out=keysT[:], in_=p_t[:, 0:2, :])
    io4_i = S.sb("io4_i", [128, 16], I32)
    io4s = S.sb("io4", [128, 16, 16], F32)
    lo16s = S.sb("lo16", [128, 16, 16], F32)
    hi16s = S.sb("hi16", [128, 16, 16], F32)
    if "i" not in SK:
        S.I("pool", "iota", [], [io4_i], io4_i[:], pattern=[[1, 16]], base=0, channel_multiplier=0)
        S.I("dve", "tensor_copy", [io4_i], [io4s], out=io4s[:], in_=io4_i[:].unsqueeze(1).to_broadcast([128, 16, 16]))
        S.I("dve", "tensor_scalar", [io4s], [lo16s], out=lo16s[:], in0=io4s[:], scalar1=16.0, scalar2=None, op0=ALU.mult)
        S.I("dve", "tensor_scalar", [io4s], [hi16s], out=hi16s[:], in0=io4s[:], scalar1=16.0, scalar2=16.0, op0=ALU.mult, op1=ALU.add)
    B4 = [128, 8, 16, 16]
    io4 = T(io4s[:].unsqueeze(1).to_broadcast(B4), io4s.res)
    lo16 = T(lo16s[:].unsqueeze(1).to_broadcast(B4), lo16s.res)
    hi16 = T(hi16s[:].unsqueeze(1).to_broadcast(B4), hi16s.res)

    xt = [S.sb("xc%d" % i, [128, 1, DM], F32) for i in range(2)]
    xnb = S.sb("xnb", [128, 1, DM], F32)
    hb = S.sb("hbc", [128, 1, DM], BF16)
    hf3 = S.sb("hf3", [128, 1, DM], F32)
    hT = S.sb("hTc", [128, 8, 128], BF16)
    qpT = S.sb("qpT", [128, 16, 128], BF16)
    Ssb = S.sb("Ssb", [128, 16, 128], F32)
    S2 = [S.sb("S2_%d" % i, [128, 128], F32) for i in range(2)]
    Tv = S.sb("Tv", [128, 16, 16], F32)
    Ti = S.sb("Ti", [128, 16, 16], U32)
    Tif = S.sb("Tif", [128, 16, 16], F32)
    cand = S.sb("cand", [128, 8, 16, 16], F32)
    cand2 = [S.sb("cand2_%d" % i, [128, 256], F32) for i in range(2)]
    Bv = S.sb("Bv", [128, 8, 16], F32)
    Bp = S.sb("Bp", [128, 8, 16], U32)
    Af = S.sb("Af", [128, 8, 16], F32)
    eq = S.sb("eq", [128, 8, 16, 16], F32)
    eq2 = S.sb("eq2", [128, 8, 16, 16], F32)
    Akf = S.sb("Akf", [128, 8, 16], F32)
    Bf = S.sb("Bf", [128, 8, 16], F32)
    Ik = S.sb("Ik", [128, 8, 16], F32)
    Jk = S.sb("Jk", [128, 8, 16], F32)
    Ef = S.sb("Ef", [128, 128], F32)
    Ei = S.sb("Ei", [128, 128], U32)
    Gt = S.sb("Gt", [128, 8, 16], F32)
    sm = S.sb("sm", [128, 8], F32)
    av = S.sb("av", [128, 128], F32)
    g1 = S.sb("g1", [128, 128], F32)
    g2 = S.sb("g2", [128, 128], F32)
    Wt = S.sb("Wt", [128, 128], F32)
    NROW = 20
    rows = [S.sb("row%d" % i, [128, 2 * DM], BF16) for i in range(NROW)]
    srow = [S.sb("srow%d" % i, [128, DM], BF16) for i in range(4)]
    nsr = 0
    p_acc = S.ps("p_acc", [128, DM], F32)
    nrow = 0
    import os
    CUT = int(os.environ.get("CUT", "99"))
    ntl = NT // 128 if ntiles is None else ntiles
    for ti in range(ntl):
        r0 = ti * 128
        x2 = xt[ti % 2]
        if CUT < 1:
            continue
        S.D("sp", x2, [d.x2], x2[:, 0, :], d.x2[r0:r0 + 128, :])
        rms_tile(S, x2, x2, 1, gf, [hb, hf3], ssq, rs, junk)
        for k in range(8):
            S.I("pe", "transpose", [hb, ident], [p_t], out=p_t[:, k, :], in_=hb[:, 0, k * 128:(k + 1) * 128], identity=ident[:])
        S.I("act", "copy", [p_t], [hT], out=hT[:], in_=p_t[:])
        for half, pp in enumerate((p_a, p_b)):
            for f in range(8):
                fc = half * 8 + f
                for k in range(8):
                    S.I("pe", "matmul", [w_pq, hT], [pp], pp[:, f, :], lhsT=w_pq[:, k, fc * 128:(fc + 1) * 128], rhs=hT[:, k, :], start=(k == 0), stop=(k == 7))
            if half == 0:
                S.I("act", "copy", [pp], [qpT], out=qpT[:, 0:8, :], in_=pp[:])
            else:
                S.I("dve", "tensor_copy", [pp], [qpT], out=qpT[:, 8:16, :], in_=pp[:])
        for half, pp in enumerate((p_a, p_b)):
            for f in range(8):
                hc = half * 8 + f
                S.I("pe", "matmul", [qpT, keysT], [pp], pp[:, f, :], lhsT=qpT[:, hc, :], rhs=keysT[:, hc % 2, :], start=True, stop=True)
            if half == 0:
                S.I("act", "copy", [pp], [Ssb], out=Ssb[:, 0:8, :], in_=pp[:])
            else:
                S.I("dve", "tensor_copy", [pp], [Ssb], out=Ssb[:, 8:16, :], in_=pp[:])
        if CUT < 2:
            continue
        for hc in range(16):
            s2 = S2[hc % 2]
            S.I("dve", "max", [Ssb], [Tv], out=Tv[:, hc, 0:8], in_=Ssb[:, hc, :])
            S.I("dve", "max_index", [Ssb, Tv], [Ti], out=Ti[:, hc, 0:8], in_max=Tv[:, hc, 0:8], in_values=Ssb[:, hc, :])
            S.I("dve", "match_replace", [Ssb, Tv], [s2], out=s2[:], in_to_replace=Tv[:, hc, 0:8], in_values=Ssb[:, hc, :], imm_value=-1e30)
            S.I("dve", "max", [s2], [Tv], out=Tv[:, hc, 8:16], in_=s2[:])
            S.I("dve", "max_index", [s2, Tv], [Ti], out=Ti[:, hc, 8:16], in_max=Tv[:, hc, 8:16], in_values=s2[:])
        if CUT < 3:
            continue
        S.I("dve", "tensor_copy", [Ti], [Tif], out=Tif[:], in_=Ti[:])
        Tv4 = Tv[:].rearrange("p (h c) k -> p h c k", c=2)
        Tif4 = Tif[:].rearrange("p (h c) k -> p h c k", c=2)
        S.I("dve", "tensor_tensor", [Tv], [cand], out=cand[:], in0=Tv4[:, :, 0, :].unsqueeze(3).to_broadcast(B4), in1=Tv4[:, :, 1, :].unsqueeze(2).to_broadcast(B4), op=ALU.add)
        for h in range(8):
            cf = cand[:, h, :, :].rearrange("p a b -> p (a b)")
            c2 = cand2[h % 2]
            S.I("dve", "max", [cand], [Bv], out=Bv[:, h, 0:8], in_=cf)
            S.I("dve", "max_index", [cand, Bv], [Bp], out=Bp[:, h, 0:8], in_max=Bv[:, h, 0:8], in_values=cf)
            S.I("dve", "match_replace", [cand, Bv], [c2], out=c2[:], in_to_replace=Bv[:, h, 0:8], in_values=cf, imm_value=-1e30)
            S.I("dve", "max", [c2], [Bv], out=Bv[:, h, 8:16], in_=c2[:])
            S.I("dve", "max_index", [c2, Bv], [Bp], out=Bp[:, h, 8:16], in_max=Bv[:, h, 8:16], in_values=c2[:])
        if CUT < 4:
            continue
        S.I("dve", "tensor_copy", [Bp], [Af], out=Af[:], in_=Bp[:])
        posb = Af[:].unsqueeze(3).to_broadcast(B4)
        S.I("dve", "tensor_tensor", [Af, lo16], [eq], out=eq[:], in0=posb, in1=lo16.t, op=ALU.is_ge)
        S.I("dve", "tensor_tensor", [Af, hi16], [eq2], out=eq2[:], in0=posb, in1=hi16.t, op=ALU.is_ge)
        S.I("dve", "tensor_tensor", [eq, eq2], [eq], out=eq[:], in0=eq[:], in1=eq2[:], op=ALU.subtract)
        S.I("dve", "tensor_tensor", [eq, io4], [eq2], out=eq2[:], in0=eq[:], in1=io4.t, op=ALU.mult)
        S.I("dve", "tensor_reduce", [eq2], [Akf], out=Akf[:], in_=eq2[:], axis=AX.X, op=ALU.add)
        S.I("dve", "tensor_tensor", [eq, Tif], [eq], out=eq[:], in0=eq[:], in1=Tif4[:, :, 0, :].unsqueeze(2).to_broadcast(B4), op=ALU.mult)
        S.I("dve", "tensor_reduce", [eq], [Ik], out=Ik[:], in_=eq[:], axis=AX.X, op=ALU.add)
        S.I("dve", "scalar_tensor_tensor", [Akf, Af], [Bf], out=Bf[:].rearrange("p h k -> p (h k)"), in0=Akf[:].rearrange("p h k -> p (h k)"), scalar=-16.0, in1=Af[:].rearrange("p h k -> p (h k)"), op0=ALU.mult, op1=ALU.add)
        S.I("dve", "tensor_tensor", [io4, Bf], [eq], out=eq[:], in0=io4.t, in1=Bf[:].unsqueeze(3).to_broadcast(B4), op=ALU.is_equal)
        S.I("dve", "tensor_tensor", [eq, Tif], [eq], out=eq[:], in0=eq[:], in1=Tif4[:, :, 1, :].unsqueeze(2).to_broadcast(B4), op=ALU.mult)
        S.I("dve", "tensor_reduce", [eq], [Jk], out=Jk[:], in_=eq[:], axis=AX.X, op=ALU.add)
        S.I("dve", "scalar_tensor_tensor", [Ik, Jk], [Ef], out=Ef[:], in0=Ik[:].rearrange("p h k -> p (h k)"), scalar=128.0, in1=Jk[:].rearrange("p h k -> p (h k)"), op0=ALU.mult, op1=ALU.add)
        S.I("dve", "tensor_copy", [Ef], [Ei], out=Ei[:], in_=Ef[:])
        if CUT < 5:
            continue
        S.I("dve", "tensor_tensor", [Bv], [Gt], out=Gt[:], in0=Bv[:], in1=Bv[:, :, 0:1].to_broadcast([128, 8, 16]), op=ALU.subtract)
        S.I("act", "activation", [Gt], [Gt], out=Gt[:], in_=Gt[:], func=AF.Exp)
        S.I("dve", "tensor_reduce", [Gt], [sm], out=sm[:], in_=Gt[:], axis=AX.X, op=ALU.add)
        S.I("dve", "reciprocal", [sm], [sm], out=sm[:], in_=sm[:])
        S.I("dve", "tensor_tensor", [Gt, sm], [Gt], out=Gt[:], in0=Gt[:], in1=sm[:].unsqueeze(2).to_broadcast([128, 8, 16]), op=ALU.mult)
        if CUT < 6:
            continue
        S.I("dve", "memset", [], [av], av[:], 0.0)
        GS = 8
        NG = 128 // GS
        Gf = Gt[:].rearrange("p h k -> p (h k)")
        grp_rows = {}

        avc = [S.res("avc") for _ in range(128)]

        def dots(g):
            nonlocal nrow
            rl = []
            for c in range(g * GS, (g + 1) * GS):
                rw = rows[nrow % NROW]
                nrow += 1
                S.dma("pool", (lambda e, rw=rw, c=c: e.indirect_dma_start(out=rw[:], out_offset=None, in_=d.euv[:, :], in_offset=bass.IndirectOffsetOnAxis(ap=Ei[:, c:c + 1], axis=0))), rw, srcs=[Ei, d.euv])
                S.I("dve", "scalar_tensor_tensor", [rw, hf3, av], [avc[c]], out=junk[:], in0=rw[:, 0:DM], scalar=1.0, in1=hf3[:, 0, :], op0=ALU.mult, op1=ALU.mult, accum_out=av[:, c:c + 1])
                rl.append(rw)
            grp_rows[g] = rl
            sl_ = slice(g * GS, (g + 1) * GS)
            ac = avc[g * GS:(g + 1) * GS]
            S.I("dve", "tensor_tensor", ac + [av], [g1g[g]], out=g1[:, sl_], in0=av[:, sl_], in1=av[:, sl_], op=ALU.mult)
            S.I("dve", "tensor_scalar", [g1g[g]], [g1g[g]], out=g1[:, sl_], in0=g1[:, sl_], scalar1=0.044715, scalar2=1.0, op0=ALU.mult, op1=ALU.add)
            S.I("dve", "tensor_tensor", [g1g[g], av] + ac, [g1g[g]], out=g1[:, sl_], in0=g1[:, sl_], in1=av[:, sl_], op=ALU.mult)
            S.I("act", "activation", [g1g[g]], [g2g[g]], out=g2[:, sl_], in_=g1[:, sl_], func=AF.Sigmoid, scale=1.5957691216057308)

        def accs(g):
            nonlocal nsr
            sl_ = slice(g * GS, (g + 1) * GS)
            ac = avc[g * GS:(g + 1) * GS]
            S.I("dve", "tensor_tensor", [g2g[g], av] + ac, [wtg[g]], out=Wt[:, sl_], in0=g2[:, sl_], in1=av[:, sl_], op=ALU.mult)
            S.I("dve", "tensor_tensor", [wtg[g], Gt], [wtg[g]], out=Wt[:, sl_], in0=Wt[:, sl_], in1=Gf[:, sl_], op=ALU.mult)
            for i, c in enumerate(range(g * GS, (g + 1) * GS)):
                rw = grp_rows[g][i]
                sr = srow[nsr % 4]
                nsr += 1
                S.I("act", "activation", [rw, wtg[g]], [sr], out=sr[:], in_=rw[:, DM:2 * DM], func=AF.Copy, scale=Wt[:, c:c + 1])
                for hf in range(2):
                    S.I("pe", "matmul", [sr, ident], [p_acc], p_acc[:, hf * 512:(hf + 1) * 512], lhsT=ident[:], rhs=sr[:, hf * 512:(hf + 1) * 512], start=(c == 0), stop=(c == 127))

        g1g = [S.res("g1g") for _ in range(NG)]
        g2g = [S.res("g2g") for _ in range(NG)]
        wtg = [S.res("wtg") for _ in range(NG)]
        dots(0)
        for g in range(NG):
            if g + 1 < NG:
                dots(g + 1)
            accs(g)
        S.I("dve", "tensor_tensor", [p_acc, x2], [x2], out=x2[:, 0, :], in0=p_acc[:], in1=x2[:, 0, :], op=ALU.add)
        if d.last:
            rms_tile(S, x2, x2, 1, gn, [xnb], ssq, rs, junk)
            S.D("sp", d.xn, [xnb], d.xn[r0:r0 + 128, :], xnb[:, 0, :])
        else:
            S.D("sp", d.xo, [x2], d.xo[r0:r0 + 128, :], x2[:, 0, :])
    if d.last:
        S.fence("sp", [d.xn])
    S.pop()


def phase3(S, d, NSEQ, ident, ntiles=None):
    import os
    sub = os.environ.get("P3", "abc")
    if "a" in sub:
        phase3a(S, d, NSEQ, ident)
    if "b" in sub:
        phase3b(S, d, NSEQ, ident, ntiles)
    if "c" in sub:
        phase3c(S, d, NSEQ, ident, ntiles)


def build_prog(NSEQ, depth=4, phases=(1, 2, 3), ntiles=None, dbg=False, heads=range(16), tts=range(4)):
    nc = bass.Bass("TRN2", target_bir_lowering=False)
    S = Sched(nc)
    d = declare(S, NSEQ, dbg, big=(3 in phases), depth=depth)
    ident = make_ident(S)
    for L in range(depth):
        v = layer_view(d, L)
        if 1 in phases:
            phase1(S, v, NSEQ, ident)
        if 2 in phases:
            phase2(S, v, NSEQ, heads, tts)
        if 3 in phases:
            phase0(S, v)
            phase3(S, v, NSEQ, ident, ntiles)
    S.emit()
    print("[build] ops:", {e: len(S.ops[e]) for e in ENGS}, "dma sems:", S.nsem, flush=True)
    S.close()
    return nc


N_CORES = 8
DEPTH = 4
_NC_CACHE = {}


def _stacked_weights(inputs):
    g = lambda k: np.ascontiguousarray(np.asarray(inputs[k]), dtype=np.float32)
    D = g("norm_mix").shape[0]
    return {
        "norm_mix": g("norm_mix").reshape(D, 1, DM),
        "w_in": g("w_in"),
        "decay": g("ret_decay_logit").reshape(D, 1, 16),
        "ret_gain": np.ascontiguousarray(g("ret_norm_gain").reshape(D, 8, 64).transpose(0, 2, 1)),
        "att_gain": np.ascontiguousarray(g("att_norm_gain").reshape(D, 8, 64).transpose(0, 2, 1)),
        "w_out": g("w_out"),
        "norm_mem": g("norm_mem").reshape(D, 1, DM),
        "norm_mem_kv": g("norm_mem_kv").reshape(D, 1, DM),
        "w_mem_q": g("w_mem_q"),
        "w_mem_kv": g("w_mem_kv"),
        "w_mem_o": g("w_mem_o"),
        "norm_ffn": g("norm_ffn").reshape(D, 1, DM),
        "peer_wq": g("peer_w_query"),
        "sub_keys": g("peer_sub_keys").reshape(D, 256, 128),
        "e_down": g("peer_expert_down"),
        "e_up": g("peer_expert_up"),
        "norm_final": g("norm_final").reshape(1, DM),
    }


def kernel(**inputs):
    x = np.asarray(inputs["x"], dtype=np.float32)
    mem = np.asarray(inputs["mem"], dtype=np.float32)
    B = x.shape[0]
    nseq = B // N_CORES
    if nseq not in _NC_CACHE:
        _NC_CACHE[nseq] = build_prog(nseq, DEPTH)
    nc = _NC_CACHE[nseq]
    w = _stacked_weights(inputs)
    in_maps = []
    for c in range(N_CORES):
        m = dict(w)
        m["x"] = np.ascontiguousarray(x[c * nseq:(c + 1) * nseq].reshape(nseq * SEQ, DM))
        m["mem"] = np.ascontiguousarray(mem[c * nseq:(c + 1) * nseq].reshape(nseq * MEM, DM))
        in_maps.append(m)
    res = run_bass_kernel_spmd(nc, in_maps, core_ids=list(range(N_CORES)))
    out = np.concatenate([np.asarray(res.results[c]["xn"]).reshape(nseq, SEQ, DM) for c in range(N_CORES)], axis=0)
    return out.astype(np.float32)
```
